# Optimizing a Trainium2 kernel written in Bass

```python
import jax
import jax.numpy as jnp
from jax import lax
import numpy as np

D_MODEL = 1024
BATCH = 8
SEQ = 4096
DEPTH = 2

N_Q_HEADS = 8
N_KV_HEADS = 2
Q_PER_KV = N_Q_HEADS // N_KV_HEADS
HEAD_DIM = 64
WINDOW = 128
ATTN_BLOCK = 128
ATTN_Q_WIDTH = N_Q_HEADS * HEAD_DIM
ATTN_KV_WIDTH = N_KV_HEADS * HEAD_DIM
N_RET_HEADS = 8
RET_KEY_DIM = 32
RET_VAL_DIM = 64
RET_CHUNK = 128
RET_QK_WIDTH = N_RET_HEADS * RET_KEY_DIM
RET_V_WIDTH = N_RET_HEADS * RET_VAL_DIM
D_FF_DENSE = 2816
N_EXPERTS = 8
TOP_K = 2
D_FF_EXPERT = 1408
N_DENSE_LAYERS = (DEPTH + 1) // 2
N_MOE_LAYERS = DEPTH // 2
IN_SIZES = (ATTN_Q_WIDTH, ATTN_KV_WIDTH, ATTN_KV_WIDTH,
            RET_QK_WIDTH, RET_QK_WIDTH, RET_V_WIDTH, RET_V_WIDTH, RET_V_WIDTH,
            2 * D_MODEL)
IN_WIDTH = sum(IN_SIZES)
DEEPNORM_ALPHA = (2 * DEPTH) ** 0.25
DEEPNORM_BETA = (8 * DEPTH) ** -0.25
LN_EPS = 1e-5
GN_EPS = 1e-5
NEG_INF = -1e30

kernel_name = 'hybrid_swa_retention_moe_deepnorm'


def layer_norm(x, g, b):
    xf = x.astype(jnp.float32)
    mu = jnp.mean(xf, axis=-1, keepdims=True)
    var = jnp.mean(jnp.square(xf - mu), axis=-1, keepdims=True)
    return ((xf - mu) * lax.rsqrt(var + LN_EPS)).astype(x.dtype) * g + b


def split_columns(proj, sizes):
    parts, start = [], 0
    for size in sizes:
        parts.append(proj[..., start:start + size])
        start += size
    return parts


def alibi_slopes(n_heads):
    return jnp.exp2(-8.0 * jnp.arange(1, n_heads + 1, dtype=jnp.float32) / n_heads)


def windowed_gqa(q, k, v, sink_logits):
    B, S, _ = q.shape
    C = ATTN_BLOCK
    nb = S // C
    qb = q.reshape(B, nb, C, N_KV_HEADS, Q_PER_KV, HEAD_DIM)
    pad = ((0, 0), (C, C), (0, 0))
    kp = jnp.pad(k, pad).reshape(B, nb + 2, C, N_KV_HEADS, HEAD_DIM)
    vp = jnp.pad(v, pad).reshape(B, nb + 2, C, N_KV_HEADS, HEAD_DIM)
    kw = jnp.concatenate([kp[:, :-2], kp[:, 1:-1], kp[:, 2:]], axis=2)
    vw = jnp.concatenate([vp[:, :-2], vp[:, 1:-1], vp[:, 2:]], axis=2)
    scores = jnp.einsum('bnqgrd,bnkgd->bngrqk', qb, kw,
                        preferred_element_type=jnp.float32) * (HEAD_DIM ** -0.5)
    qi = jnp.arange(C)
    kj = jnp.arange(3 * C)
    dist = jnp.abs(qi[:, None] - kj[None, :] + C)
    key_pos = (jnp.arange(nb)[:, None] - 1) * C + kj[None, :]
    valid = (dist <= WINDOW)[None] & ((key_pos >= 0) & (key_pos < S))[:, None, :]
    slopes = alibi_slopes(N_Q_HEADS).reshape(N_KV_HEADS, Q_PER_KV)
    scores = scores - slopes[:, :, None, None] * dist.astype(jnp.float32)
    scores = jnp.where(valid[None, :, None, None], scores, NEG_INF)
    sink = jnp.broadcast_to(
        sink_logits.astype(jnp.float32).reshape(N_KV_HEADS, Q_PER_KV)[:, :, None, None],
        scores.shape[:-1] + (1,))
    probs = jax.nn.softmax(jnp.concatenate([scores, sink], axis=-1), axis=-1)[..., :-1]
    out = jnp.einsum('bngrqk,bnkgd->bnqgrd', probs.astype(v.dtype), vw)
    return out.reshape(B, S, ATTN_Q_WIDTH)


def head_group_norm(y):
    mu = jnp.mean(y, axis=-1, keepdims=True)
    var = jnp.mean(jnp.square(y - mu), axis=-1, keepdims=True)
    return (y - mu) * lax.rsqrt(var + GN_EPS)


def retention_one_direction(q, k, v, log_gamma, include_diag):
    B, S, H, dk = q.shape
    dv = v.shape[-1]
    C = RET_CHUNK
    nc = S // C
    qc = q.reshape(B, nc, C, H, dk)
    kc = k.reshape(B, nc, C, H, dk)
    vc = v.reshape(B, nc, C, H, dv)
    pos_i = jnp.arange(C)
    diff_i = pos_i[:, None] - pos_i[None, :]
    mask = (diff_i >= 0) if include_diag else (diff_i > 0)
    pos = pos_i.astype(jnp.float32)
    diff = jnp.maximum(diff_i, 0).astype(jnp.float32)
    intra_decay = jnp.where(mask[None], jnp.exp(diff[None] * log_gamma[:, None, None]), 0.0)
    q_decay = jnp.exp((pos + 1.0)[None] * log_gamma[:, None])
    k_decay = jnp.exp((C - 1.0 - pos)[None] * log_gamma[:, None])
    chunk_decay = jnp.exp(C * log_gamma)
    qk = jnp.einsum('bnqhd,bnkhd->bnhqk', qc, kc) * intra_decay
    intra = jnp.einsum('bnhqk,bnkhe->bnqhe', qk, vc)
    chunk_kv = jnp.einsum('bnkhd,hk,bnkhe->bnhde', kc, k_decay, vc)

    def step(state, kv):
        return chunk_decay[None, :, None, None] * state + kv, state

    init = jnp.zeros((B, H, dk, dv), jnp.float32)
    _, prev = lax.scan(step, init, jnp.moveaxis(chunk_kv, 1, 0))
    prev = jnp.moveaxis(prev, 0, 1)
    cross = jnp.einsum('bnqhd,bnhde,hq->bnqhe', qc, prev, q_decay)
    return (intra + cross).reshape(B, S, H, dv)


def bidirectional_retention(q, k, v, g_f, g_b, decay_fwd, decay_bwd):
    B, S, _ = q.shape
    qh = q.reshape(B, S, N_RET_HEADS, RET_KEY_DIM).astype(jnp.float32)
    kh = k.reshape(B, S, N_RET_HEADS, RET_KEY_DIM).astype(jnp.float32) * (RET_KEY_DIM ** -0.5)
    vh = v.reshape(B, S, N_RET_HEADS, RET_VAL_DIM).astype(jnp.float32)
    lg_f = jax.nn.log_sigmoid(decay_fwd.astype(jnp.float32))
    lg_b = jax.nn.log_sigmoid(decay_bwd.astype(jnp.float32))
    y_f = retention_one_direction(qh, kh, vh, lg_f, True)
    y_b = jnp.flip(retention_one_direction(jnp.flip(qh, 1), jnp.flip(kh, 1), jnp.flip(vh, 1),
                                           lg_b, False), axis=1)
    n_f = head_group_norm(y_f).reshape(B, S, RET_V_WIDTH).astype(g_f.dtype)
    n_b = head_group_norm(y_b).reshape(B, S, RET_V_WIDTH).astype(g_b.dtype)
    return jax.nn.silu(g_f) * n_f + jax.nn.silu(g_b) * n_b


def hybrid_mixer(h, w_in, b_gate, sink_logits, decay_fwd, decay_bwd, w_o_attn, w_o_ret, w_out):
    q_a, k_a, v_a, q_r, k_r, v_r, g_f, g_b, gate_logits = split_columns(h @ w_in, IN_SIZES)
    y_attn = windowed_gqa(q_a, k_a, v_a, sink_logits) @ w_o_attn
    y_ret = bidirectional_retention(q_r, k_r, v_r, g_f, g_b, decay_fwd, decay_bwd) @ w_o_ret
    gates = jax.nn.sigmoid(gate_logits + b_gate)
    merged = gates[..., :D_MODEL] * y_attn + gates[..., D_MODEL:] * y_ret
    return merged @ w_out


def swiglu(x, w1, w3, w2):
    return (jax.nn.silu(x @ w1) * (x @ w3)) @ w2


def moe_swiglu(x, router_w, router_b, w1, w3, w2):
    logits = (x @ router_w).astype(jnp.float32) + router_b.astype(jnp.float32)
    top_vals, top_idx = lax.top_k(logits, TOP_K)
    top_w = jax.nn.softmax(top_vals, axis=-1)
    gate = jnp.sum(jax.nn.one_hot(top_idx, N_EXPERTS, dtype=jnp.float32) * top_w[..., None], axis=-2)
    gate = gate.astype(x.dtype)
    out = jnp.zeros_like(x)
    for e in range(N_EXPERTS):
        out = out + gate[..., e:e + 1] * swiglu(x, w1[e], w3[e], w2[e])
    return out


def setup_inputs(seed: int = 0) -> dict:
    key = jax.random.key(seed)
    ks = jax.random.split(key, 24)
    f32 = jnp.float32
    beta = DEEPNORM_BETA

    def normal(k, shape, scale):
        return jax.random.normal(k, shape, f32) * scale

    col_mult = (1.0, 1.0, beta, 1.0, 1.0, beta, 1.0, 1.0, 1.0)
    col_scale = jnp.concatenate([jnp.full((s,), c, f32) for s, c in zip(IN_SIZES, col_mult)])
    base_gamma = 1.0 - jnp.exp2(-5.0 - jnp.arange(N_RET_HEADS, dtype=f32))
    base_logit = jnp.log(base_gamma) - jnp.log1p(-base_gamma)
    return {
        'x': normal(ks[0], (BATCH, SEQ, D_MODEL), 1.0),
        'ln_emb_g': 1.0 + normal(ks[1], (D_MODEL,), 0.02),
        'ln_emb_b': normal(ks[2], (D_MODEL,), 0.02),
        'w_in': normal(ks[3], (DEPTH, D_MODEL, IN_WIDTH), D_MODEL ** -0.5) * col_scale,
        'b_gate': normal(ks[4], (DEPTH, 2 * D_MODEL), 0.02),
        'sink_logits': normal(ks[5], (DEPTH, N_Q_HEADS), 0.5),
        'decay_fwd': base_logit + normal(ks[6], (DEPTH, N_RET_HEADS), 0.1),
        'decay_bwd': base_logit + normal(ks[7], (DEPTH, N_RET_HEADS), 0.1),
        'w_o_attn': normal(ks[8], (DEPTH, ATTN_Q_WIDTH, D_MODEL), beta * ATTN_Q_WIDTH ** -0.5),
        'w_o_ret': normal(ks[9], (DEPTH, RET_V_WIDTH, D_MODEL), beta * RET_V_WIDTH ** -0.5),
        'w_out': normal(ks[10], (DEPTH, D_MODEL, D_MODEL), beta * D_MODEL ** -0.5),
        'ln1_g': 1.0 + normal(ks[11], (DEPTH, D_MODEL), 0.02),
        'ln1_b': normal(ks[12], (DEPTH, D_MODEL), 0.02),
        'ffn_w1': normal(ks[13], (N_DENSE_LAYERS, D_MODEL, D_FF_DENSE), beta * D_MODEL ** -0.5),
        'ffn_w3': normal(ks[14], (N_DENSE_LAYERS, D_MODEL, D_FF_DENSE), beta * D_MODEL ** -0.5),
        'ffn_w2': normal(ks[15], (N_DENSE_LAYERS, D_FF_DENSE, D_MODEL), beta * D_FF_DENSE ** -0.5),
        'router_w': normal(ks[16], (N_MOE_LAYERS, D_MODEL, N_EXPERTS), D_MODEL ** -0.5),
        'router_b': normal(ks[17], (N_MOE_LAYERS, N_EXPERTS), 0.01),
        'moe_w1': normal(ks[18], (N_MOE_LAYERS, N_EXPERTS, D_MODEL, D_FF_EXPERT), beta * D_MODEL ** -0.5),
        'moe_w3': normal(ks[19], (N_MOE_LAYERS, N_EXPERTS, D_MODEL, D_FF_EXPERT), beta * D_MODEL ** -0.5),
        'moe_w2': normal(ks[20], (N_MOE_LAYERS, N_EXPERTS, D_FF_EXPERT, D_MODEL), beta * D_FF_EXPERT ** -0.5),
        'ln2_g': 1.0 + normal(ks[21], (DEPTH, D_MODEL), 0.02),
        'ln2_b': normal(ks[22], (DEPTH, D_MODEL), 0.02),
    }


def reference(x, ln_emb_g, ln_emb_b, w_in, b_gate, sink_logits, decay_fwd, decay_bwd,
              w_o_attn, w_o_ret, w_out, ln1_g, ln1_b, ffn_w1, ffn_w3, ffn_w2,
              router_w, router_b, moe_w1, moe_w3, moe_w2, ln2_g, ln2_b):
    x = layer_norm(x, ln_emb_g, ln_emb_b)
    for layer in range(DEPTH):
        mix = hybrid_mixer(x, w_in[layer], b_gate[layer], sink_logits[layer],
                           decay_fwd[layer], decay_bwd[layer],
                           w_o_attn[layer], w_o_ret[layer], w_out[layer])
        x = layer_norm(DEEPNORM_ALPHA * x + mix, ln1_g[layer], ln1_b[layer])
        i = layer // 2
        if layer % 2 == 0:
            ffn = swiglu(x, ffn_w1[i], ffn_w3[i], ffn_w2[i])
        else:
            ffn = moe_swiglu(x, router_w[i], router_b[i], moe_w1[i], moe_w3[i], moe_w2[i])
        x = layer_norm(DEEPNORM_ALPHA * x + ffn, ln2_g[layer], ln2_b[layer])
    return x
```

```python
import numpy as np
import concourse.bass as bass
import concourse.mybir as mybir
from concourse.bass_utils import run_bass_kernel_spmd

F32 = mybir.dt.float32
BF16 = mybir.dt.bfloat16
AF = mybir.ActivationFunctionType
ALU = mybir.AluOpType
AX = mybir.AxisListType

ENGINES = ("sync", "scalar", "gpsimd", "vector", "tensor")
EPOCH = 12000

S = 4096
D = 1024
NCH = 32
C = 128
DEPTH = 2
ALPHA = (2 * DEPTH) ** 0.25
LN_EPS = 1e-5
GN_EPS = 1e-5
FD = 2816
FE = 1408
NE = 8
TG = 1024
NTG = S // TG
TPG = TG // 128


class Buf:
    __slots__ = ("name", "last_w", "readers", "psum")

    def __init__(self, name):
        self.name = name
        self.last_w = None
        self.readers = []
        self.psum = False


class Op:
    __slots__ = ("eng", "fn", "deps", "is_dma", "ndma", "key", "signal", "sem_key", "sem_val")

    def __init__(self, eng, fn, is_dma, ndma, key):
        self.eng = eng
        self.fn = fn
        self.deps = []
        self.is_dma = is_dma
        self.ndma = ndma
        self.key = key
        self.signal = False
        self.sem_key = None
        self.sem_val = None


class Prog:
    def __init__(self, nc):
        self.nc = nc
        self.ops = {e: [] for e in ENGINES}
        self.all_ops = []

    def _add(self, op, reads, writes):
        deps = []
        for b in reads:
            if b.last_w is not None:
                deps.append(b.last_w)
            if b.psum:
                deps.extend(r for r in b.readers if r.eng != op.eng)
        for b in writes:
            if b.last_w is not None:
                deps.append(b.last_w)
            deps.extend(b.readers)
        out = []
        seen = set()
        for d in deps:
            if d is op or id(d) in seen:
                continue
            seen.add(id(d))
            if (not d.is_dma) and d.eng == op.eng and not op.is_dma:
                if op.eng == "tensor":
                    continue
                raw = any(b.last_w is d for b in reads)
                if not raw:
                    continue
            out.append(d)
        op.deps = out
        for b in reads:
            b.readers.append(op)
        for b in writes:
            b.last_w = op
            b.readers = []
        self.ops[op.eng].append(op)
        self.all_ops.append(op)
        return op

    def op(self, eng, fn, reads=(), writes=()):
        return self._add(Op(eng, fn, False, 0, None), list(reads), list(writes))

    def dma(self, eng, fn, reads=(), writes=(), key=None, ndma=1):
        if key is None:
            key = writes[0] if writes else reads[0]
        return self._add(Op(eng, fn, True, ndma, key), list(reads), list(writes))

    def fence(self):
        targets = []
        for e in ENGINES:
            for op in reversed(self.ops[e]):
                if not op.is_dma and op.fn is not None:
                    targets.append(op)
                    break
        lastdma = {}
        for op in self.all_ops:
            if op.is_dma:
                lastdma[id(op.key)] = op
        targets += list(lastdma.values())
        for e in ENGINES:
            f = Op(e, None, False, 0, None)
            f.deps = [t for t in targets if not (t.eng == e and e == "tensor" and not t.is_dma)]
            self.ops[e].append(f)
            self.all_ops.append(f)

    def emit(self):
        nc = self.nc
        for op in self.all_ops:
            for d in op.deps:
                d.signal = True
        sems = {}
        cnt = {}
        for e in ENGINES:
            n = 0
            for op in self.ops[e]:
                if op.is_dma:
                    k = ("dma", id(op.key))
                    cnt[k] = cnt.get(k, 0) + 16 * op.ndma
                    op.sem_key = k
                    op.sem_val = cnt[k]
                elif op.signal:
                    n += 1
                    ep, v = divmod(n - 1, EPOCH)
                    op.sem_key = (e, ep)
                    op.sem_val = v + 1
        for op in self.all_ops:
            if op.sem_key is not None and op.sem_key not in sems:
                sems[op.sem_key] = nc.alloc_semaphore("s%d" % len(sems))
        self.nsems = len(sems)

        def run_engine(e):
            def body(eng):
                known = {}
                for op in self.ops[e]:
                    need = {}
                    for d in op.deps:
                        k, v = d.sem_key, d.sem_val
                        if known.get(k, 0) >= v:
                            continue
                        if need.get(k, 0) < v:
                            need[k] = v
                    for k, v in need.items():
                        eng.wait_ge(sems[k], v)
                        known[k] = v
                    if op.fn is None:
                        continue
                    if op.is_dma:
                        op.fn(eng, sems[op.sem_key])
                    else:
                        inst = op.fn(eng)
                        if op.signal:
                            inst.then_inc(sems[op.sem_key], 1)
                last = {}
                for op in self.ops[e]:
                    if op.is_dma:
                        last[op.sem_key] = op.sem_val
                for k, v in last.items():
                    if known.get(k, 0) < v:
                        eng.wait_ge(sems[k], v)
            return body

        with nc.Block() as block:
            for e in ENGINES:
                if self.ops[e]:
                    getattr(block, e)(run_engine(e))


class T:
    __slots__ = ("t", "b")

    def __init__(self, t, name):
        self.t = t
        self.b = Buf(name)

    def __getitem__(self, k):
        return self.t[k]


def host_constants():
    k = np.arange(128, dtype=np.float32)[:, None]
    q = np.arange(128, dtype=np.float32)[None, :]
    cst = {}
    slopes = np.exp2(-8.0 * np.arange(1, 9, dtype=np.float32) / 8.0)
    am = np.zeros((128, 3, 8, 128), np.float32)
    for j in range(3):
        if j == 0:
            dist = q - k + 128.0
        elif j == 1:
            dist = np.abs(q - k)
        else:
            dist = k + 128.0 - q
        valid = (dist <= 128.0).astype(np.float32)
        for h in range(8):
            am[:, j, h, :] = np.exp(-slopes[h] * dist) * valid
    cst["c_amask"] = am.reshape(128, 3 * 8 * 128)
    sc = 32.0 ** -0.5
    ef = np.maximum(q - k, 0.0)
    mf = (q >= k).astype(np.float32) * sc
    eb = np.maximum(k - q, 0.0)
    mb = (k > q).astype(np.float32) * sc
    qp1 = np.broadcast_to(q + 1.0, (128, 128))
    qm = np.broadcast_to(128.0 - q, (128, 128))
    cst["c_ret"] = np.ascontiguousarray(np.concatenate([ef, mf, eb, mb, qp1, qm], axis=1).astype(np.float32))
    kp = np.concatenate([127.0 - k, k], axis=1)
    cst["c_kpos"] = np.ascontiguousarray(kp.astype(np.float32))
    return cst


def build_program(stop="full"):
    nc = bass.Bass("TRN2", target_bir_lowering=False)
    P = Prog(nc)
    import os
    DBG = set(os.environ.get("K_DBG", "").split(","))

    def din(name, shape, dt=F32):
        return nc.dram_tensor(name, list(shape), dt, kind="ExternalInput").ap()

    x_in = din("x", [S, D])
    y_out = nc.dram_tensor("y", [S, D], F32, kind="ExternalOutput").ap()
    w_in = din("w_in", [DEPTH, D, 4864])
    w_oa = din("w_o_attn", [DEPTH, 512, D])
    w_or = din("w_o_ret", [DEPTH, 512, D])
    w_out = din("w_out", [DEPTH, D, D])
    ffn_w1 = din("ffn_w1", [1, D, FD])
    ffn_w3 = din("ffn_w3", [1, D, FD])
    ffn_w2 = din("ffn_w2", [1, FD, D])
    router_w = din("router_w", [1, D, NE])
    moe_w1 = din("moe_w1", [1, NE, D, FE])
    moe_w3 = din("moe_w3", [1, NE, D, FE])
    moe_w2 = din("moe_w2", [1, NE, FE, D])
    ln_tab = din("ln_tab", [10, 128, D])
    bgate = din("bgate", [DEPTH, 1, 2048])
    dec_bc = din("dec_bc", [128, DEPTH * 2 * 8])
    dec_pp = din("dec_pp", [128, DEPTH * 2 * 2])
    sink_bc = din("sink_bc", [128, DEPTH * 8])
    rb_bc = din("rb_bc", [128, NE])
    c_amask = din("c_amask", [128, 3 * 8 * 128])
    c_ret = din("c_ret", [128, 6 * 128])
    c_kpos = din("c_kpos", [128, 2])

    xs0 = nc.dram_tensor("xs0", [S, D], F32).ap()
    xs1 = nc.dram_tensor("xs1", [S, D], F32).ap()
    x1t = nc.dram_tensor("x1t", [NCH, 128, 1024], BF16).ap()
    art = nc.dram_tensor("art", [NCH, 128, 1024], BF16).ap()
    gts = nc.dram_tensor("gts", [S, 2048], BF16).ap()
    b_xs0 = [Buf("xs0_%d" % i) for i in range(NCH)]
    b_xs1 = [Buf("xs1_%d" % i) for i in range(NCH)]
    b_x1t = [Buf("x1t_%d" % i) for i in range(NCH)]
    b_art = [Buf("art_%d" % i) for i in range(NCH)]
    b_gts = [Buf("gts_%d" % i) for i in range(NCH)]
    b_y = [Buf("y%d" % i) for i in range(NCH)]

    cur = {"ph": None, "lo": None, "hi": None}
    ptrs = {}

    def sb(name, shape, dt):
        ph = cur["ph"]
        if ph is None:
            assert cur["lo"] is None
            return T(nc.alloc_sbuf_tensor(name, list(shape), dt), name)
        size = int(np.prod(shape[1:])) * mybir.dt.size(dt)
        size = (size + 31) // 32 * 32
        off = ptrs.get(ph, cur["lo"])
        assert off + size <= cur["hi"], "SBUF arena overflow in phase %s at %s (%d > %d)" % (ph, name, off + size, cur["hi"])
        ptrs[ph] = off + size
        return T(nc.alloc_sbuf_tensor_at(name, list(shape), dt, offset=off), name)

    def open_arena():
        lo, hi = nc.bump_sbuf(nc.sbuf_bytes_remaining - 64)
        cur["lo"], cur["hi"] = lo, hi

    ps_f = [T(nc.alloc_psum_tensor("psf%d" % i, [128, 512], F32), "psf%d" % i) for i in range(6)]
    ps_b = [T(nc.alloc_psum_tensor("psb%d" % i, [128, 1024], BF16), "psb%d" % i) for i in range(2)]
    for t_ in ps_f + ps_b:
        t_.b.psum = True
    ring = {"f": 0, "b": 0}

    def bank():
        t = ps_f[ring["f"] % 6]
        ring["f"] += 1
        return t

    def tbank():
        t = ps_b[ring["b"] % 2]
        ring["b"] += 1
        return t

    def mm(out, lhsT, rhs, start, stop, reads, writes, tp=None):
        if tp is None:
            P.op("tensor", lambda e: e.matmul(out, lhsT, rhs, start=start, stop=stop), reads, writes)
        else:
            P.op("tensor", lambda e: e.matmul(out, lhsT, rhs, start=start, stop=stop, tile_position=tp),
                 reads, writes)

    def tr(out, in_, reads, writes):
        P.op("tensor", lambda e: e.transpose(out, in_, ident[:]), list(reads) + [ident.b], writes)

    def act(out, in_, func, reads, writes, scale=1.0, bias=0.0):
        P.op("scalar", lambda e: e.activation(out, in_, func, bias=bias, scale=scale), reads, writes)

    def vcopy(eng, out, in_, reads, writes):
        P.op(eng, lambda e: e.tensor_copy(out, in_), reads, writes)

    def tt(eng, out, a, b, op, reads, writes):
        P.op(eng, lambda e: e.tensor_tensor(out, a, b, op), reads, writes)

    def ts(eng, out, a, s1, s2, op0, op1, reads, writes):
        if s2 is None:
            P.op(eng, lambda e: e.tensor_scalar(out, a, s1, None, op0), reads, writes)
        else:
            P.op(eng, lambda e: e.tensor_scalar(out, a, s1, s2, op0, op1), reads, writes)

    def stt(eng, out, in0, scalar, in1, op0, op1, reads, writes):
        P.op(eng, lambda e: e.scalar_tensor_tensor(out, in0, scalar, in1, op0, op1), reads, writes)

    def load(eng, out_ap, in_ap, writes, reads=(), key=None):
        P.dma(eng, lambda e, s: e.dma_start(out=out_ap, in_=in_ap).then_inc(s, 16), reads, writes, key=key)

    def store(eng, out_ap, in_ap, reads, writes, key=None):
        P.dma(eng, lambda e, s: e.dma_start(out=out_ap, in_=in_ap).then_inc(s, 16), reads, writes, key=key)

    ident = sb("ident", [128, 128], BF16)
    identf = sb("identf", [128, 128], F32)
    ones1 = sb("ones1", [1, 128], BF16)
    P.op("vector", lambda e: e.memset(identf[:], 1.0), [], [identf.b])
    P.op("gpsimd", lambda e: e.affine_select(out=identf[:], in_=identf[:], pattern=[[-1, 128]],
                                             compare_op=ALU.is_equal, fill=0.0, base=0, channel_multiplier=1),
         [identf.b], [identf.b])
    vcopy("vector", ident[:], identf[:], [identf.b], [ident.b])
    P.op("vector", lambda e: e.memset(ones1[:], 1.0), [], [ones1.b])

    lnp = sb("lnp", [128, 2, D], F32)

    def load_ln(idx):
        load("sync", lnp[:, 0, :], ln_tab[idx], [lnp.b])
        load("sync", lnp[:, 1, :], ln_tab[idx + 1], [lnp.b])

    NLN = 8
    ln_sts = [sb("ln_st%d" % i, [128, 2, 6], F32) for i in range(NLN)]
    ln_mvs = [sb("ln_mv%d" % i, [128, 2], F32) for i in range(NLN)]
    ln_rss = [sb("ln_rs%d" % i, [128, 1], F32) for i in range(NLN)]
    ln_nbs = [sb("ln_nb%d" % i, [128, 1], F32) for i in range(NLN)]
    ln_i = {"i": 0}
    epsln = sb("epsln", [128, 1], F32)
    mhalf = sb("mhalf", [128, 8], F32)
    mhalf16 = sb("mhalf16", [128, 16], F32)
    P.op("vector", lambda e: e.memset(epsln[:], LN_EPS), [], [epsln.b])
    P.op("vector", lambda e: e.memset(mhalf[:], -0.5), [], [mhalf.b])
    P.op("vector", lambda e: e.memset(mhalf16[:], -0.5), [], [mhalf16.b])

    def ln_stages(src, dst, rd, wr, beta_eng="gpsimd", split_pow=False):
        k_ = ln_i["i"] % NLN
        ln_i["i"] += 1
        ln_st, ln_mv, ln_rs, ln_nb = ln_sts[k_], ln_mvs[k_], ln_rss[k_], ln_nbs[k_]

        def s1():
            P.op("vector", lambda e: e.bn_stats(ln_st[:, 0, :], src[:, 0:512]), rd, [ln_st.b])
            P.op("vector", lambda e: e.bn_stats(ln_st[:, 1, :], src[:, 512:1024]), rd, [ln_st.b])
            P.op("vector", lambda e: e.bn_aggr(ln_mv[:], ln_st[:]), [ln_st.b], [ln_mv.b])
            ts("vector", ln_rs[:], ln_mv[:, 1:2], LN_EPS, None, ALU.add, None, [ln_mv.b], [ln_rs.b])
            if not split_pow:
                s1b()

        def s1b():
            tt("gpsimd", ln_rs[:], ln_rs[:], mhalf[:, 0:1], ALU.pow, [ln_rs.b, mhalf.b], [ln_rs.b])

        def s2():
            stt("vector", ln_nb[:], ln_mv[:, 0:1], -1.0, ln_rs[:], ALU.mult, ALU.mult, [ln_mv.b, ln_rs.b], [ln_nb.b])
            P.op("scalar", lambda e: e.activation(dst, src, AF.Identity, bias=ln_nb[:], scale=ln_rs[:]),
                 list(rd) + [ln_rs.b, ln_nb.b], wr)

        def s3():
            tt("vector", dst, dst, lnp[:, 0, :], ALU.mult, list(wr) + [lnp.b], wr)
            tt(beta_eng, dst, dst, lnp[:, 1, :], ALU.add, list(wr) + [lnp.b], wr)

        if split_pow:
            return s1, s1b, s2, s3
        return s1, s2, s3

    def layer_norm(src, dst, rd, wr, beta_eng="gpsimd"):
        for f in ln_stages(src, dst, rd, wr, beta_eng):
            f()

    amask = sb("amask", [128, 3, 8, 128], BF16)
    cret = sb("cret", [128, 6, 128], F32)
    ckpos = sb("ckpos", [128, 2], F32)
    decb = sb("decb", [128, DEPTH * 2 * 8], F32)
    decp = sb("decp", [128, DEPTH * 2 * 2], F32)
    lgb = sb("lgb", [128, DEPTH * 2 * 8], F32)
    lgp = sb("lgp", [128, DEPTH * 2 * 2], F32)
    sinkb = sb("sinkb", [128, DEPTH * 8], F32)
    sinke = sb("sinke", [128, DEPTH * 8], F32)
    open_arena()

    cur["ph"] = "P0"
    FUSE_P0 = stop != "x0"
    load_ln(0)
    p0_in = [sb("p0in%d" % i, [128, D], F32) for i in range(4)]
    p0_out = [sb("p0out%d" % i, [128, D], F32) for i in range(4)]
    for n in range(0 if FUSE_P0 else 2):
        load("sync", p0_in[n % 4][:], x_in[n * 128:(n + 1) * 128, :], [p0_in[n % 4].b])
    for n in range(0 if FUSE_P0 else NCH):
        xi, xo = p0_in[n % 4], p0_out[n % 4]
        if n + 2 < NCH:
            load("sync", p0_in[(n + 2) % 4][:], x_in[(n + 2) * 128:(n + 3) * 128, :], [p0_in[(n + 2) % 4].b])
        layer_norm(xi[:], xo[:], [xi.b], [xo.b])
        dst = y_out if stop == "x0" else xs0
        store("sync", dst[n * 128:(n + 1) * 128, :], xo[:], [xo.b], [b_xs0[n]], key=xo.b)
    if stop == "x0":
        P.emit()
        return nc

    cur["ph"] = "MIX"
    Wtm = sb("Wtm", [128, 8, 1920], BF16)
    WtmA_b = Buf("WtmA")
    WtmB_b = Buf("WtmB")
    Wg = sb("Wg", [128, 8, 2048], BF16)
    Wfm = sb("Wfm", [128, 8, 1152], BF16)
    bgr = sb("bgr", [1, 2048], BF16)
    vtmp = [sb("vtmp%d" % i, [128, 512], F32) for i in range(2)]
    vmean = sb("vmean", [128, 64], F32)
    maskF = sb("maskF", [128, 8, 128], BF16)
    maskB = sb("maskB", [128, 8, 128], BF16)
    mtmp = sb("mtmp", [128, 128], F32)
    QF = sb("QF", [128, 2, 128], F32)
    QB = sb("QB", [128, 2, 128], F32)
    KF = sb("KF", [128, 8], F32)
    KBt = sb("KBt", [128, 8], F32)
    GCF = sb("GCF", [128, 2], F32)
    GCB = sb("GCB", [128, 2], F32)
    prevB = sb("prevB", [128, NCH, 2, 64], BF16)
    Sf = sb("Sf", [128, 2, 64], F32)
    Sb_ = sb("Sb", [128, 2, 64], F32)
    SfBF = sb("SfBF", [128, 2, 64], BF16)

    load("gpsimd", amask[:].rearrange("p j h q -> p (j h q)"), c_amask, [amask.b])
    load("sync", cret[:].rearrange("p a q -> p (a q)"), c_ret, [cret.b])
    load("sync", ckpos[:], c_kpos, [ckpos.b])
    load("sync", decb[:], dec_bc, [decb.b])
    load("sync", decp[:], dec_pp, [decp.b])
    load("sync", sinkb[:], sink_bc, [sinkb.b])
    for src_, dst_ in ((decb, lgb), (decp, lgp)):
        act(dst_[:], src_[:], AF.Exp, [src_.b], [dst_.b], scale=-1.0)
        act(dst_[:], dst_[:], AF.Ln, [dst_.b], [dst_.b], bias=1.0)
        ts("vector", dst_[:], dst_[:], -1.0, None, ALU.mult, None, [dst_.b], [dst_.b])
    act(sinke[:], sinkb[:], AF.Exp, [sinkb.b], [sinke.b])

    SC = 32.0 ** -0.5

    def mixer_setup(l):
        P.fence()
        for v in Vx:
            P.op("vector", lambda e, v=v: e.memset(v[:], 1.0), [], [v.b])
        wl = w_in[l].rearrange("(c p) n -> p c n", p=128)
        P.dma("gpsimd", lambda e, s: (
            e.dma_start(out=Wtm[:, :, 0:128], in_=wl[:, :, 640:768]).then_inc(s, 16),
            e.dma_start(out=Wtm[:, :, 128:384], in_=wl[:, :, 1024:1280]).then_inc(s, 16)),
            [], [WtmA_b], ndma=2)
        P.dma("gpsimd", lambda e, s: e.dma_start(out=Wtm[:, :, 896:1920], in_=wl[:, :, 1792:2816]).then_inc(s, 16),
              [], [WtmB_b])
        load("gpsimd", Wg[:], wl[:, :, 2816:4864], [Wg.b])

        def wfm_loads(e, s):
            for g in range(2):
                for c in range(8):
                    e.dma_start(
                        out=Wfm[:, c, 0:512].rearrange("p (r g2 e) -> p r g2 e", r=4, g2=2)[:, :, g, :],
                        in_=wl[:, c, g * 256:(g + 1) * 256].rearrange("p (r e) -> p r e", r=4)).then_inc(s, 16)
            e.dma_start(out=Wfm[:, :, 512:640], in_=wl[:, :, 512:640]).then_inc(s, 16)
            e.dma_start(out=Wfm[:, :, 640:1152], in_=wl[:, :, 768:1280]).then_inc(s, 16)
        P.dma("gpsimd", wfm_loads, [], [Wfm.b], ndma=18)
        load("gpsimd", bgr[:], bgate[l], [bgr.b])
        for c in range(8):
            vt = vtmp[c % 2]
            load("sync", vt[:], wl[:, c, 1280:1792], [vt.b])
            P.op("vector", lambda e, vt=vt, c=c: e.reduce_sum(vmean[:, c * 8:(c + 1) * 8],
                                                          vt[:].rearrange("p (h e) -> p h e", e=64), axis=AX.X),
                 [vt.b], [vmean.b])
            ts("vector", vmean[:, c * 8:(c + 1) * 8], vmean[:, c * 8:(c + 1) * 8], -1.0 / 64.0, None, ALU.mult, None,
               [vmean.b], [vmean.b])
            tt("vector", Wtm[:, c, 384:896].rearrange("p (h e) -> p h e", e=64),
               vt[:].rearrange("p (h e) -> p h e", e=64),
               vmean[:, c * 8:(c + 1) * 8].unsqueeze(2).to_broadcast([128, 8, 64]), ALU.add,
               [vt.b, vmean.b], [WtmA_b])
        of, ob = (l * 2 + 0) * 8, (l * 2 + 1) * 8
        for h in range(8):
            act(mtmp[:], cret[:, 0, :], AF.Exp, [cret.b, lgb.b], [mtmp.b], scale=lgb[:, of + h:of + h + 1])
            tt("vector", maskF[:, h, :], mtmp[:], cret[:, 1, :], ALU.mult, [mtmp.b, cret.b], [maskF.b])
            act(mtmp[:], cret[:, 2, :], AF.Exp, [cret.b, lgb.b], [mtmp.b], scale=lgb[:, ob + h:ob + h + 1])
            tt("vector", maskB[:, h, :], mtmp[:], cret[:, 3, :], ALU.mult, [mtmp.b, cret.b], [maskB.b])
        pf, pb = (l * 2 + 0) * 2, (l * 2 + 1) * 2
        for hh in range(2):
            act(QF[:, hh, :], cret[:, 4, :], AF.Exp, [cret.b, lgp.b], [QF.b], scale=lgp[:, pf + hh:pf + hh + 1])
            act(QB[:, hh, :], cret[:, 5, :], AF.Exp, [cret.b, lgp.b], [QB.b], scale=lgp[:, pb + hh:pb + hh + 1])
        act(KF[:], lgb[:, of:of + 8], AF.Exp, [lgb.b, ckpos.b], [KF.b], scale=ckpos[:, 0:1])
        ts("vector", KF[:], KF[:], SC, None, ALU.mult, None, [KF.b], [KF.b])
        act(KBt[:], lgb[:, ob:ob + 8], AF.Exp, [lgb.b, ckpos.b], [KBt.b], scale=ckpos[:, 1:2])
        ts("vector", KBt[:], KBt[:], SC, None, ALU.mult, None, [KBt.b], [KBt.b])
        act(GCF[:], lgp[:, pf:pf + 2], AF.Exp, [lgp.b], [GCF.b], scale=128.0)
        act(GCB[:], lgp[:, pb:pb + 2], AF.Exp, [lgp.b], [GCB.b], scale=128.0)

    xin = [sb("xin%d" % i, [128, D], F32) for i in range(3)]
    xbf = [sb("xbf%d" % i, [128, D], BF16) for i in range(3)]
    hT = [sb("hT%d" % i, [128, 8, 128], BF16) for i in range(3)]
    vr = [sb("vr%d" % i, [128, 512], BF16) for i in range(2)]
    kt = [sb("kt%d" % i, [128, 256], BF16) for i in range(2)]
    sgx = [sb("sgx%d" % i, [128, 2, 512], BF16) for i in range(2)]
    gsb = [sb("gsb%d" % i, [128, 2048], BF16) for i in range(2)]
    qaT = [sb("qaT%d" % i, [128, 4, 128], BF16) for i in range(2)]
    qrR = [sb("qrR%d" % i, [128, 2, 128], BF16) for i in range(2)]
    qrF = [sb("qrF%d" % i, [128, 2, 128], BF16) for i in range(2)]
    qrB = [sb("qrB%d" % i, [128, 2, 128], BF16) for i in range(2)]
    krT = [sb("krT%d" % i, [128, 2, 128], BF16) for i in range(2)]
    KTc = [sb("KTc%d" % i, [128, 128], BF16) for i in range(4)]
    Vx = [sb("Vx%d" % i, [128, 2, 65], BF16) for i in range(4)]
    Ebuf = [sb("Ebuf%d" % i, [128, 512], BF16) for i in range(3)]
    Pm = [sb("Pm%d" % i, [128, 4, 128], BF16) for i in range(6)]
    den = sb("den", [128, 8], F32)
    Abf = sb("Abf", [128, 8, 64], BF16)
    ARs = [sb("ARs%d" % i, [128, 8, 128], BF16) for i in range(2)]
    SFm = sb("SFm", [128, 8, 128], BF16)
    SBm = sb("SBm", [128, 8, 128], BF16)
    sq = sb("sq", [128, 2, 512], F32)
    ssq = sb("ssq", [128, 16], F32)
    sgr = sb("sgr", [128, 2, 512], F32)
    tfb = sb("tfb", [128, 2, 512], F32)
    ycp = sb("ycp", [128, 2, 512], F32)
    Rbf = sb("Rbf", [128, 512], BF16)

    def load_chunk_hT(n, src, bsrc, slot):
        xi, xb_, h = xin[slot], xbf[slot], hT[slot]
        load("sync", xi[:], src[n * 128:(n + 1) * 128, :], [xi.b], reads=[bsrc[n]])
        vcopy("vector", xb_[:], xi[:], [xi.b], [xb_.b])
        tb = tbank()
        for c in range(8):
            tr(tb[:, c * 128:(c + 1) * 128], xb_[:, c * 128:(c + 1) * 128], [xb_.b], [tb.b])
        vcopy("vector", h[:].rearrange("p c t -> p (c t)"), tb[:], [tb.b], [h.b])

    def proj_tm(h, W, lo, hi, bias=None):
        bk = bank()
        n = hi - lo
        wbuf = W.b
        if W is Wtm:
            wbuf = WtmA_b if hi <= 896 else WtmB_b
        for c in range(8):
            mm(bk[:, 0:n], h[:, c, :], W[:, c, lo:hi], c == 0, (c == 7 and bias is None), [h.b, wbuf], [bk.b])
        if bias is not None and "nobias" not in DBG:
            mm(bk[:, 0:n], ones1[:], bias, False, True, [ones1.b, bgr.b], [bk.b])
        return bk

    def state_update(St, GC, ktile, vtile):
        bk = bank()
        for h in range(8):
            hh, hl = divmod(h, 4)
            mm(bk[hl * 32:(hl + 1) * 32, hh * 64:(hh + 1) * 64], ktile[:, h * 32:(h + 1) * 32],
               vtile[:, h * 64:(h + 1) * 64], True, True, [ktile.b, vtile.b], [bk.b], tp=(0, hl * 32))
        for hh in range(2):
            stt("vector", St[:, hh, :], St[:, hh, :], GC[:, hh:hh + 1], bk[:, hh * 64:(hh + 1) * 64],
                ALU.mult, ALU.add, [St.b, GC.b, bk.b], [St.b])

    def mixer_M1(l, src, bsrc, fuse_ln=False):
        P.op("vector", lambda e: e.memset(Sb_[:], 0.0), [], [Sb_.b])
        order = list(range(NCH - 1, -1, -1))
        raw = x_in if fuse_ln else src

        def ld(n):
            xi = xin[n % 3]
            if fuse_ln:
                load("sync", xi[:], raw[n * 128:(n + 1) * 128, :], [xi.b])
            else:
                load("sync", xi[:], raw[n * 128:(n + 1) * 128, :], [xi.b], reads=[bsrc[n]])

        def prep_stages(n):
            xi, xb_ = xin[n % 3], xbf[n % 3]
            if fuse_ln:
                s1, s2, s3 = ln_stages(xi[:], xi[:], [xi.b], [xi.b])
            else:
                s1 = s2 = s3 = (lambda: None)

            def s3b():
                s3()
                if fuse_ln:
                    store("sync", src[n * 128:(n + 1) * 128, :], xi[:], [xi.b], [bsrc[n]], key=xi.b)
                vcopy("vector", xb_[:], xi[:], [xi.b], [xb_.b])
            return s1, s2, s3b

        for j in range(3):
            ld(order[j])
            for f in prep_stages(order[j]):
                f()
        ld(order[3])
        A_h2(l, order[0])
        for i, n in enumerate(order):
            slot = n % 2
            if i + 1 < NCH:
                A_h2(l, order[i + 1])
            st = prep_stages(order[i + 3]) if i + 3 < NCH else None
            if st:
                st[0]()
            h = hT[n % 3]
            bk_k = proj_tm(h, Wtm, 128, 384)
            bk_v = proj_tm(h, Wtm, 384, 896)
            tt("vector", kt[slot][:].rearrange("p (h d) -> p h d", d=32),
               bk_k[:, 0:256].rearrange("p (h d) -> p h d", d=32),
               KBt[:].unsqueeze(2).to_broadcast([128, 8, 32]), ALU.mult, [bk_k.b, KBt.b], [kt[slot].b])
            act(vr[slot][:], bk_v[:], AF.Copy, [bk_v.b], [vr[slot].b])
            if st:
                st[1]()
            vcopy("vector", prevB[:, n, :, :], Sb_[:], [Sb_.b], [prevB.b])
            state_update(Sb_, GCB, kt[slot], vr[slot])
            if st:
                st[2]()
            if i + 4 < NCH:
                ld(order[i + 4])

    def A_h1(l, n, src, bsrc):
        xi, xb_ = xin[n % 3], xbf[n % 3]
        load("sync", xi[:], src[n * 128:(n + 1) * 128, :], [xi.b], reads=[bsrc[n]])
        vcopy("vector", xb_[:], xi[:], [xi.b], [xb_.b])

    def A_h2(l, n):
        xb_, h = xbf[n % 3], hT[n % 3]
        tb = tbank()
        for c in range(8):
            tr(tb[:, c * 128:(c + 1) * 128], xb_[:, c * 128:(c + 1) * 128], [xb_.b], [tb.b])
        vcopy("vector", h[:].rearrange("p c t -> p (c t)"), tb[:], [tb.b], [h.b])

    def A_kv(l, n):
        slot = n % 2
        h = hT[n % 3]
        bk = bank()
        for c in range(8):
            mm(bk[:, 0:128], Wfm[:, c, 512:640], h[:, c, :], c == 0, c == 7, [Wfm.b, h.b], [bk.b])
        vcopy("vector", KTc[n % 4][:], bk[:, 0:128], [bk.b], [KTc[n % 4].b])
        bk = proj_tm(h, Wtm, 0, 384)
        vcopy("vector", Vx[n % 4][:, :, 0:64], bk[:, 0:128].rearrange("p (g e) -> p g e", e=64), [bk.b], [Vx[n % 4].b])
        tt("vector", kt[slot][:].rearrange("p (h d) -> p h d", d=32),
           bk[:, 128:384].rearrange("p (h d) -> p h d", d=32),
           KF[:].unsqueeze(2).to_broadcast([128, 8, 32]), ALU.mult, [bk.b, KF.b], [kt[slot].b])

    def A_rest(l, n):
        slot = n % 2
        h = hT[n % 3]
        bk = proj_tm(h, Wtm, 384, 896)
        act(vr[slot][:], bk[:], AF.Copy, [bk.b], [vr[slot].b])
        bk = proj_tm(h, Wtm, 896, 1408)
        act(sgx[slot][:, 0, :], bk[:], AF.Silu, [bk.b], [sgx[slot].b])
        bk = proj_tm(h, Wtm, 1408, 1920)
        act(sgx[slot][:, 1, :], bk[:], AF.Silu, [bk.b], [sgx[slot].b])
        for j in range(4):
            bk = proj_tm(h, Wg, j * 512, (j + 1) * 512, bias=bgr[:, j * 512:(j + 1) * 512])
            act(gsb[slot][:, j * 512:(j + 1) * 512], bk[:], AF.Sigmoid, [bk.b], [gsb[slot].b])
        store("sync", gts[n * 128:(n + 1) * 128, :], gsb[slot][:], [gsb[slot].b], [b_gts[n]], key=gsb[slot].b)
        bk = bank()
        for blk in range(4):
            for c in range(8):
                mm(bk[:, blk * 128:(blk + 1) * 128], Wfm[:, c, blk * 128:(blk + 1) * 128], h[:, c, :],
                   c == 0, c == 7, [Wfm.b, h.b], [bk.b])
        vcopy("vector", qaT[slot][:].rearrange("p r t -> p (r t)"), bk[:], [bk.b], [qaT[slot].b])
        bk = bank()
        for blk in range(4):
            for c in range(8):
                mm(bk[:, blk * 128:(blk + 1) * 128], Wfm[:, c, 640 + blk * 128:640 + (blk + 1) * 128], h[:, c, :],
                   c == 0, c == 7, [Wfm.b, h.b], [bk.b])
        act(qrR[slot][:].rearrange("p a t -> p (a t)"), bk[:, 0:256], AF.Copy, [bk.b], [qrR[slot].b])
        act(krT[slot][:].rearrange("p a t -> p (a t)"), bk[:, 256:512], AF.Copy, [bk.b], [krT[slot].b])
        tt("vector", qrF[slot][:].rearrange("p a t -> p (a t)"), bk[:, 0:256], QF[:].rearrange("p a t -> p (a t)"),
           ALU.mult, [bk.b, QF.b], [qrF[slot].b])
        tt("vector", qrB[slot][:].rearrange("p a t -> p (a t)"), bk[:, 0:256], QB[:].rearrange("p a t -> p (a t)"),
           ALU.mult, [bk.b, QB.b], [qrB[slot].b])

    pm_i = {"i": 0, "e": 0}
    pms_of = {}
    ybk_of = {}

    def S1(l, n):
        slot = n % 2
        for hl in range(4):
            bk = bank()
            for hh in range(2):
                mm(bk[:, hh * 128:(hh + 1) * 128], krT[slot][hl * 32:(hl + 1) * 32, hh, :],
                   qrR[slot][hl * 32:(hl + 1) * 32, hh, :], True, True, [krT[slot].b, qrR[slot].b], [bk.b],
                   tp=(hl * 32, 0))
            bv = bk[:, 0:256].rearrange("p (h t) -> p h t", t=128)
            tt("vector", SFm[:, hl:8:4, :], bv, maskF[:, hl:8:4, :], ALU.mult, [bk.b, maskF.b], [SFm.b])
            tt("vector", SBm[:, hl:8:4, :], bv, maskB[:, hl:8:4, :], ALU.mult, [bk.b, maskB.b], [SBm.b])
        js = [j for j in range(3) if 0 <= n - 1 + j < NCH]
        pms = {}
        for j in js:
            kc = KTc[(n - 1 + j) % 4]
            for g in range(2):
                bk = bank()
                mm(bk[:], kc[g * 64:(g + 1) * 64, :], qaT[slot][g * 64:(g + 1) * 64].rearrange("p r t -> p (r t)"),
                   True, True, [kc.b, qaT[slot].b], [bk.b], tp=(g * 64, 0))
                eb = Ebuf[pm_i["e"] % 3]
                pm_i["e"] += 1
                act(eb[:], bk[:], AF.Exp, [bk.b], [eb.b], scale=0.125)
                pm = Pm[pm_i["i"] % 6]
                pm_i["i"] += 1
                tt("gpsimd", pm[:], eb[:].rearrange("p (r t) -> p r t", t=128), amask[:, j, g * 4:(g + 1) * 4, :],
                   ALU.mult, [eb.b, amask.b], [pm.b])
                pms[(j, g)] = pm
        pms_of[n] = (js, pms)

    def S2(l, n):
        slot = n % 2
        js, pms = pms_of.pop(n)
        for hb in range(2):
            bk = bank()
            for r in range(4):
                for ji, j in enumerate(js):
                    vx = Vx[(n - 1 + j) % 4]
                    pm = pms[(j, hb)]
                    mm(bk[:, r * 65:(r + 1) * 65], pm[:, r, :], vx[:, hb, :], ji == 0, ji == len(js) - 1,
                       [pm.b, vx.b], [bk.b])
            ov = bk[:, 0:260].rearrange("p (r e) -> p r e", e=65)
            tt("vector", den[:, hb * 4:(hb + 1) * 4], ov[:, :, 64], sinke[:, l * 8 + hb * 4:l * 8 + (hb + 1) * 4],
               ALU.add, [bk.b, sinke.b], [den.b])
            P.op("vector", lambda e, hb=hb: e.reciprocal(den[:, hb * 4:(hb + 1) * 4], den[:, hb * 4:(hb + 1) * 4]),
                 [den.b], [den.b])
            tt("vector", Abf[:, hb * 4:(hb + 1) * 4, :], ov[:, :, 0:64],
               den[:, hb * 4:(hb + 1) * 4].unsqueeze(2).to_broadcast([128, 4, 64]), ALU.mult,
               [bk.b, den.b], [Abf.b])
        ybk = []
        for d, (Sm, qx, has_cross) in enumerate(((SFm, qrF[slot], n > 0), (SBm, qrB[slot], n < NCH - 1))):
            bk = bank()
            for h in range(8):
                hh, hl = divmod(h, 4)
                mm(bk[:, h * 64:(h + 1) * 64], Sm[:, h, :], vr[slot][:, h * 64:(h + 1) * 64], True, not has_cross,
                   [Sm.b, vr[slot].b], [bk.b])
                if has_cross:
                    if d == 0:
                        rhs, rb = SfBF[hl * 32:(hl + 1) * 32, hh, :], SfBF.b
                    else:
                        rhs, rb = prevB[hl * 32:(hl + 1) * 32, n, hh, :], prevB.b
                    mm(bk[:, h * 64:(h + 1) * 64], qx[hl * 32:(hl + 1) * 32, hh, :], rhs, False, True,
                       [qx.b, rb], [bk.b], tp=(hl * 32, 0))
            ybk.append(bk)
        if n < NCH - 1:
            state_update(Sf, GCF, kt[slot], vr[slot])
            vcopy("vector", SfBF[:], Sf[:], [Sf.b], [SfBF.b])
        for d in range(2):
            bk = ybk[d]
            act(ycp[:, d, :], bk[:], AF.Copy, [bk.b], [ycp.b])
            act(sq[:, d, :], bk[:], AF.Square, [bk.b], [sq.b])
        P.op("vector", lambda e: e.reduce_sum(ssq[:], sq[:].rearrange("p d (h e) -> p (d h) e", e=64), axis=AX.X),
             [sq.b], [ssq.b])
        ts("vector", ssq[:], ssq[:], 1.0 / 64.0, GN_EPS, ALU.mult, ALU.add, [ssq.b], [ssq.b])
        tt("gpsimd", ssq[:], ssq[:], mhalf16[:], ALU.pow, [ssq.b, mhalf16.b], [ssq.b])
        tt("gpsimd", sgr[:].rearrange("p d (h e) -> p (d h) e", e=64),
           sgx[slot][:].rearrange("p d (h e) -> p (d h) e", e=64),
           ssq[:].unsqueeze(2).to_broadcast([128, 16, 64]), ALU.mult, [sgx[slot].b, ssq.b], [sgr.b])

    def S2b(l, n):
        tt("vector", tfb[:].rearrange("p d f -> p (d f)"), ycp[:].rearrange("p d f -> p (d f)"),
           sgr[:].rearrange("p d f -> p (d f)"), ALU.mult, [ycp.b, sgr.b], [tfb.b])
        tt("gpsimd", Rbf[:], tfb[:, 0, :], tfb[:, 1, :], ALU.add, [tfb.b], [Rbf.b])

    def S3(l, n):
        ars = ARs[n % 2]
        tb = tbank()
        af = Abf[:].rearrange("p h e -> p (h e)")
        for c in range(4):
            tr(tb[:, c * 128:(c + 1) * 128], af[:, c * 128:(c + 1) * 128], [Abf.b], [tb.b])
        for c in range(4):
            tr(tb[:, 512 + c * 128:512 + (c + 1) * 128], Rbf[:, c * 128:(c + 1) * 128], [Rbf.b], [tb.b])
        vcopy("vector", ars[:].rearrange("p c t -> p (c t)"), tb[:], [tb.b], [ars.b])
        store("sync", art[n], ars[:].rearrange("p c t -> p (c t)"), [ars.b], [b_art[n]],
              key=ars.b)

    def mixer_M2(l, src, bsrc):
        P.op("vector", lambda e: e.memset(Sf[:], 0.0), [], [Sf.b])
        A_h1(l, 0, src, bsrc)
        A_h1(l, 1, src, bsrc)
        A_h1(l, 2, src, bsrc)
        A_h2(l, 0)
        A_h2(l, 1)
        A_kv(l, 0)
        A_rest(l, 0)
        for n in range(NCH):
            if n + 2 < NCH:
                A_h2(l, n + 2)
            if n >= 1:
                S2b(l, n - 1)
            if n + 1 < NCH:
                A_kv(l, n + 1)
            S1(l, n)
            if n >= 1:
                S3(l, n - 1)
            if n + 1 < NCH:
                A_rest(l, n + 1)
            if n + 3 < NCH:
                A_h1(l, n + 3, src, bsrc)
            S2(l, n)
        S2b(l, NCH - 1)
        S3(l, NCH - 1)

    cur["ph"] = "O"
    Woa = sb("Woa", [128, 4, D], BF16)
    Wor = sb("Wor", [128, 4, D], BF16)
    Wo = sb("Wo", [128, 8, D], BF16)
    NO = 4
    o_ar = [sb("o_ar%d" % i, [128, 8, 128], BF16) for i in range(NO)]
    o_g = [sb("o_g%d" % i, [128, 2048], BF16) for i in range(NO)]
    o_x = [sb("o_x%d" % i, [128, D], F32) for i in range(NO)]
    o_m1 = [sb("o_m1%d" % i, [128, 512], F32) for i in range(4)]
    o_m2 = [sb("o_m2%d" % i, [128, 512], F32) for i in range(4)]
    o_mb = [sb("o_mb%d" % i, [128, D], BF16) for i in range(NO)]
    o_mT = [sb("o_mT%d" % i, [128, 8, 128], BF16) for i in range(NO)]
    NO2 = 6
    o_r = [sb("o_r%d" % i, [128, D], F32) for i in range(NO2)]
    o_x1 = [sb("o_x1%d" % i, [128, D], F32) for i in range(NO2)]
    o_x1b = [sb("o_x1b%d" % i, [128, D], BF16) for i in range(NO2)]
    o_x1T = [sb("o_x1T%d" % i, [128, 8, 128], BF16) for i in range(NO2)]
    om_i = {"i": 0}

    def phase_O(l, src, bsrc):
        P.fence()
        load("gpsimd", Woa[:], w_oa[l].rearrange("(c p) n -> p c n", p=128), [Woa.b])
        load("gpsimd", Wor[:], w_or[l].rearrange("(c p) n -> p c n", p=128), [Wor.b])
        load("gpsimd", Wo[:], w_out[l].rearrange("(c p) n -> p c n", p=128), [Wo.b])
        load_ln(2 + 2 * l)

        def OL(t):
            k = t % NO
            ar, g, xi = o_ar[k], o_g[k], o_x[k]
            load("sync", ar[:].rearrange("p c t -> p (c t)"), art[t], [ar.b], reads=[b_art[t]])
            load("sync", g[:], gts[t * 128:(t + 1) * 128, :], [g.b], reads=[b_gts[t]])
            load("sync", xi[:], src[t * 128:(t + 1) * 128, :], [xi.b], reads=[bsrc[t]])

        def OY(t):
            k = t % NO
            ar, g, xi = o_ar[k], o_g[k], o_x[k]
            for cb in range(2):
                bka = bank()
                for c in range(4):
                    mm(bka[:], ar[:, c, :], Woa[:, c, cb * 512:(cb + 1) * 512], c == 0, c == 3, [ar.b, Woa.b], [bka.b])
                bkr = bank()
                for c in range(4):
                    mm(bkr[:], ar[:, 4 + c, :], Wor[:, c, cb * 512:(cb + 1) * 512], c == 0, c == 3, [ar.b, Wor.b], [bkr.b])
                m1, m2 = o_m1[om_i["i"] % 4], o_m2[om_i["i"] % 4]
                om_i["i"] += 1
                tt("vector", m1[:], bka[:], g[:, cb * 512:(cb + 1) * 512], ALU.mult, [bka.b, g.b], [m1.b])
                tt("vector", m2[:], bkr[:], g[:, 1024 + cb * 512:1024 + (cb + 1) * 512], ALU.mult, [bkr.b, g.b], [m2.b])
                tt("gpsimd", o_mb[k][:, cb * 512:(cb + 1) * 512], m1[:], m2[:], ALU.add, [m1.b, m2.b], [o_mb[k].b])

        ln_of = {}

        def OT(t):
            k = t % NO
            k2 = t % NO2
            xi = o_x[k]
            tb = tbank()
            for c in range(8):
                tr(tb[:, c * 128:(c + 1) * 128], o_mb[k][:, c * 128:(c + 1) * 128], [o_mb[k].b], [tb.b])
            act(o_mT[k][:].rearrange("p c t -> p (c t)"), tb[:], AF.Copy, [tb.b], [o_mT[k].b])
            for cb in range(2):
                bk = bank()
                for c in range(8):
                    mm(bk[:], o_mT[k][:, c, :], Wo[:, c, cb * 512:(cb + 1) * 512], c == 0, c == 7, [o_mT[k].b, Wo.b], [bk.b])
                stt("vector", o_r[k2][:, cb * 512:(cb + 1) * 512], xi[:, cb * 512:(cb + 1) * 512], ALPHA, bk[:],
                    ALU.mult, ALU.add, [xi.b, bk.b], [o_r[k2].b])
            x1 = o_x1[k2]
            ln_of[t] = ln_stages(o_r[k2][:], x1[:], [o_r[k2].b], [x1.b], beta_eng="vector")
            ln_of[t][0]()

        def OL2(t):
            ln_of[t][1]()

        def OL3(t):
            k2 = t % NO2
            x1 = o_x1[k2]
            ln_of.pop(t)[2]()
            dst = y_out if stop == "l%d_x1" % l else xs1
            store("sync", dst[t * 128:(t + 1) * 128, :], x1[:], [x1.b], [b_xs1[t]], key=x1.b)
            act(o_x1b[k2][:], x1[:], AF.Copy, [x1.b], [o_x1b[k2].b])

        def OX(t):
            k2 = t % NO2
            tb = tbank()
            for c in range(8):
                tr(tb[:, c * 128:(c + 1) * 128], o_x1b[k2][:, c * 128:(c + 1) * 128], [o_x1b[k2].b], [tb.b])
            xT = o_x1T[k2]
            vcopy("vector", xT[:].rearrange("p c t -> p (c t)"), tb[:], [tb.b], [xT.b])
            store("sync", x1t[t], xT[:].rearrange("p c t -> p (c t)"), [xT.b], [b_x1t[t]],
                  key=xT.b)

        OL(0)
        OL(1)
        for i in range(NCH + 5):
            if i + 2 < NCH:
                OL(i + 2)
            if i < NCH:
                OY(i)
            if 0 <= i - 1 < NCH:
                OT(i - 1)
            if 0 <= i - 2 < NCH:
                OL2(i - 2)
            if 0 <= i - 3 < NCH:
                OL3(i - 3)
            if 0 <= i - 4 < NCH:
                OX(i - 4)

    cur["ph"] = "FF"
    X1T = [sb("X1T%d" % i, [128, TPG, 8, 128], BF16) for i in range(2)]
    XTb = [[Buf("XTb%d_%d" % (i, t)) for t in range(TPG)] for i in range(2)]
    ACC = sb("ACC", [128, TPG, D], F32)
    ACCb = [Buf("ACC%d" % t) for t in range(TPG)]
    hTb = [sb("hTb%d" % i, [128, 11, TG], BF16) for i in range(2)]
    WA = [sb("WA%d" % i, [128, 8, 2, 128], BF16) for i in range(3)]
    WB = [sb("WB%d" % i, [128, 11, D], BF16) for i in range(2)]
    stmp = [sb("stmp%d" % i, [128, 512], BF16) for i in range(2)]
    gate = [sb("gate%d" % i, [128, TPG, NE], F32) for i in range(2)]
    f_x2 = [sb("f_x2%d" % i, [128, D], F32) for i in range(2)]
    Wr = sb("Wr", [128, 8, NE], F32)
    rbb = sb("rbb", [128, NE], F32)
    r_x = [sb("r_x%d" % i, [128, D], F32) for i in range(2)]
    r_xT = sb("r_xT", [128, 8, 128], F32)
    r_lg = sb("r_lg", [128, NE], F32)
    r_m1 = sb("r_m1", [128, 1], F32)
    r_m2 = sb("r_m2", [128, 1], F32)
    r_k1 = sb("r_k1", [128, NE], F32)
    r_k2 = sb("r_k2", [128, NE], F32)
    r_l2 = sb("r_l2", [128, NE], F32)
    r_d = sb("r_d", [128, 1], F32)
    r_w1 = sb("r_w1", [128, 1], F32)
    r_w2 = sb("r_w2", [128, 1], F32)
    wa_i = {"i": 0, "b": 0, "h": 0, "s": 0}

    def phase_FF(l, is_moe, dst, bdst):
        P.fence()
        load_ln(6 + 2 * l)
        if is_moe:
            load("sync", Wr[:], router_w[0].rearrange("(c p) e -> p c e", p=128), [Wr.b])
            load("sync", rbb[:], rb_bc, [rbb.b])
            experts = [(moe_w1[0, e], moe_w3[0, e], moe_w2[0, e], 0, e) for e in range(NE)]
        else:
            experts = [(ffn_w1[0], ffn_w3[0], ffn_w2[0], hf * FE, None) for hf in range(2)]

        def ff_inputs(G):
            t0 = G * TPG
            X, gt = X1T[G % 2], gate[G % 2]
            for t in range(TPG):
                load("sync", X[:, t, :, :].rearrange("p c t -> p (c t)"), x1t[t0 + t], [XTb[G % 2][t]],
                     reads=[b_x1t[t0 + t]])
            if not is_moe:
                return
            for t in range(TPG):
                rx = r_x[t % 2]
                load("sync", rx[:], xs1[(t0 + t) * 128:(t0 + t + 1) * 128, :], [rx.b], reads=[b_xs1[t0 + t]])
                for half in range(2):
                    bk = bank()
                    for c in range(4):
                        cc = half * 4 + c
                        P.op("tensor", lambda e, bk=bk, c=c, cc=cc, rx=rx: e.transpose(
                            bk[:, c * 128:(c + 1) * 128], rx[:, cc * 128:(cc + 1) * 128], identf[:]),
                            [rx.b, identf.b], [bk.b])
                    vcopy("vector", r_xT[:, half * 4:(half + 1) * 4, :].rearrange("p c t -> p (c t)"), bk[:],
                          [bk.b], [r_xT.b])
                bk = bank()
                for c in range(8):
                    mm(bk[:, 0:NE], r_xT[:, c, :], Wr[:, c, :], c == 0, c == 7, [r_xT.b, Wr.b], [bk.b])
                tt("vector", r_lg[:], bk[:, 0:NE], rbb[:], ALU.add, [bk.b, rbb.b], [r_lg.b])
                P.op("vector", lambda e: e.reduce_max(r_m1[:], r_lg[:], axis=AX.X), [r_lg.b], [r_m1.b])
                ts("vector", r_k1[:], r_lg[:], r_m1[:], None, ALU.is_equal, None, [r_lg.b, r_m1.b], [r_k1.b])
                stt("vector", r_l2[:], r_k1[:], -1e30, r_lg[:], ALU.mult, ALU.add, [r_k1.b, r_lg.b], [r_l2.b])
                P.op("vector", lambda e: e.reduce_max(r_m2[:], r_l2[:], axis=AX.X), [r_l2.b], [r_m2.b])
                ts("vector", r_k2[:], r_l2[:], r_m2[:], None, ALU.is_equal, None, [r_l2.b, r_m2.b], [r_k2.b])
                tt("vector", r_d[:], r_m1[:], r_m2[:], ALU.subtract, [r_m1.b, r_m2.b], [r_d.b])
                act(r_w1[:], r_d[:], AF.Sigmoid, [r_d.b], [r_w1.b])
                act(r_w2[:], r_d[:], AF.Sigmoid, [r_d.b], [r_w2.b], scale=-1.0)
                ts("vector", r_k1[:], r_k1[:], r_w1[:], None, ALU.mult, None, [r_k1.b, r_w1.b], [r_k1.b])
                stt("vector", gt[:, t, :], r_k2[:], r_w2[:], r_k1[:], ALU.mult, ALU.add,
                    [r_k2.b, r_w2.b, r_k1.b], [gt.b])

        def acc_init_tile(G, t):
            t0 = G * TPG
            ax = r_x[t % 2]
            load("sync", ax[:], xs1[(t0 + t) * 128:(t0 + t + 1) * 128, :], [ax.b], reads=[b_xs1[t0 + t]])
            act(ACC[:, t, :], ax[:], AF.Identity, [ax.b], [ACCb[t]], scale=ALPHA)

        def epilogue_tile(G, t):
            for f in epilogue_stages(G, t):
                f()

        def epilogue_stages(G, t):
            t0 = G * TPG
            x2 = f_x2[t % 2]
            s1, s1b, s2, s3 = ln_stages(ACC[:, t, :], x2[:], [ACCb[t]], [x2.b], beta_eng="vector", split_pow=True)

            def s2b():
                s1b()
                s2()

            def s3b():
                s3()
                store("sync", dst[(t0 + t) * 128:(t0 + t + 1) * 128, :], x2[:], [x2.b], [bdst[t0 + t]], key=x2.b)
            return s1, s2b, s3b

        def expert(G, w1, w3, w2, foff, eidx, hooks=None):
            X, gt, xb_ = X1T[G % 2], gate[G % 2], XTb[G % 2]
            hb_ = hTb[wa_i["h"] % 2]
            wa_i["h"] += 1
            wb = WB[wa_i["b"] % 2]
            wa_i["b"] += 1
            load("gpsimd", wb[:], w2[foff:foff + FE, :].rearrange("(c p) n -> p c n", p=128), [wb.b])
            w1v = w1.rearrange("(c p) f -> p c f", p=128)
            w3v = w3.rearrange("(c p) f -> p c f", p=128)
            for fb in range(11):
                wa = WA[wa_i["i"] % 3]
                wa_i["i"] += 1
                f0 = foff + fb * 128
                P.dma("gpsimd", lambda e, s, wa=wa, f0=f0, w1v=w1v, w3v=w3v: (
                    e.dma_start(out=wa[:, :, 0, :], in_=w1v[:, :, f0:f0 + 128]).then_inc(s, 16),
                    e.dma_start(out=wa[:, :, 1, :], in_=w3v[:, :, f0:f0 + 128]).then_inc(s, 16)),
                    [], [wa.b], ndma=2)
                for tg in range(TG // 512):
                    xr = X[:, tg * 4:(tg + 1) * 4, :, :]
                    xbs = xb_[tg * 4:(tg + 1) * 4]
                    b1 = bank()
                    for c in range(8):
                        mm(b1[:].rearrange("p (a t) -> p a t", t=128), wa[:, c, 0, :], xr[:, :, c, :], c == 0, c == 7,
                           [wa.b] + xbs, [b1.b])
                    b3 = bank()
                    for c in range(8):
                        mm(b3[:].rearrange("p (a t) -> p a t", t=128), wa[:, c, 1, :], xr[:, :, c, :], c == 0, c == 7,
                           [wa.b] + xbs, [b3.b])
                    st = stmp[wa_i["s"] % 2]
                    wa_i["s"] += 1
                    act(st[:], b1[:], AF.Silu, [b1.b], [st.b])
                    tt("vector", hb_[:, fb, tg * 512:(tg + 1) * 512], st[:], b3[:], ALU.mult, [st.b, b3.b], [hb_.b])
                if hooks is not None:
                    for fn in hooks.get(fb, ()):
                        fn()
            for t in range(TPG):
                for cb in range(2):
                    bk = bank()
                    for fc in range(11):
                        mm(bk[:], hb_[:, fc, t * 128:(t + 1) * 128], wb[:, fc, cb * 512:(cb + 1) * 512],
                           fc == 0, fc == 10, [hb_.b, wb.b], [bk.b])
                    acc = ACC[:, t, cb * 512:(cb + 1) * 512]
                    if eidx is None:
                        tt("vector", acc, bk[:], acc, ALU.add, [bk.b, ACCb[t]], [ACCb[t]])
                    else:
                        stt("vector", acc, bk[:], gt[:, t, eidx:eidx + 1], acc, ALU.mult, ALU.add,
                            [bk.b, gt.b, ACCb[t]], [ACCb[t]])

        ff_inputs(0)
        for t in range(TPG):
            acc_init_tile(0, t)
        for G in range(NTG):
            for ei, (w1, w3, w2, foff, eidx) in enumerate(experts):
                hooks = None
                if ei == 0 and G > 0:
                    hooks = {}
                    for t in range(TPG):
                        e1, e2, e3 = epilogue_stages(G - 1, t)
                        hooks.setdefault(t, []).append(e1)
                        hooks.setdefault(t + 1, []).append(e2)
                        hooks.setdefault(t + 2, []).append(e3)
                        hooks.setdefault(t + 3, []).append(lambda G=G, t=t: acc_init_tile(G, t))
                expert(G, w1, w3, w2, foff, eidx, hooks)
                if ei == 0 and G + 1 < NTG:
                    ff_inputs(G + 1)
        for t in range(TPG):
            epilogue_tile(NTG - 1, t)

    for l in range(DEPTH):
        mixer_setup(l)
        if stop == "dbg_setup":
            break
        mixer_M1(l, xs0, b_xs0, fuse_ln=(l == 0))
        if stop == "dbg_m1":
            break
        mixer_M2(l, xs0, b_xs0)
        if stop == "dbg_m2":
            break
        phase_O(l, xs0, b_xs0)
        if stop == "l%d_x1" % l:
            break
        last = (l == DEPTH - 1) or stop == "l%d_x2" % l
        phase_FF(l, l % 2 == 1, y_out if last else xs0, b_y if last else b_xs0)
        if last:
            break
    P.emit()
    if os.environ.get("K_VERBOSE"):
        print("arena", cur["lo"], cur["hi"], {k: v - cur["lo"] for k, v in ptrs.items()}, "sems", P.nsems,
              {e: len(P.ops[e]) for e in ENGINES})
    return nc


_CACHE = {}


def _host_inputs(inputs):
    f = lambda a: np.ascontiguousarray(np.asarray(a, dtype=np.float32))
    shared = {}
    for k in ("w_in", "w_o_attn", "w_o_ret", "w_out", "ffn_w1", "ffn_w3", "ffn_w2", "router_w",
              "moe_w1", "moe_w3", "moe_w2"):
        shared[k] = f(inputs[k])
    rows = [inputs["ln_emb_g"], inputs["ln_emb_b"],
            inputs["ln1_g"][0], inputs["ln1_b"][0], inputs["ln1_g"][1], inputs["ln1_b"][1],
            inputs["ln2_g"][0], inputs["ln2_b"][0], inputs["ln2_g"][1], inputs["ln2_b"][1]]
    shared["ln_tab"] = f(np.stack([np.broadcast_to(np.asarray(r, np.float32)[None, :], (128, D)) for r in rows]))
    shared["bgate"] = f(np.asarray(inputs["b_gate"]).reshape(DEPTH, 1, 2048))
    dec = np.stack([np.asarray(inputs["decay_fwd"], np.float32), np.asarray(inputs["decay_bwd"], np.float32)], axis=1)
    shared["dec_bc"] = f(np.broadcast_to(dec.reshape(1, -1), (128, DEPTH * 2 * 8)))
    pp = np.zeros((128, DEPTH, 2, 2), np.float32)
    for p in range(128):
        for hh in range(2):
            pp[p, :, :, hh] = dec[:, :, hh * 4 + p // 32]
    shared["dec_pp"] = f(pp.reshape(128, -1))
    shared["sink_bc"] = f(np.broadcast_to(np.asarray(inputs["sink_logits"], np.float32).reshape(1, -1), (128, DEPTH * 8)))
    shared["rb_bc"] = f(np.broadcast_to(np.asarray(inputs["router_b"], np.float32).reshape(1, -1), (128, NE)))
    shared.update(host_constants())
    return shared


def kernel(**inputs):
    stop = inputs.pop("_stop", "full")
    ncores = inputs.pop("_ncores", 8)
    if stop not in _CACHE:
        _CACHE[stop] = build_program(stop)
    nc = _CACHE[stop]
    shared = _host_inputs(inputs)
    x = np.asarray(inputs["x"], dtype=np.float32)
    in_maps = []
    for b in range(ncores):
        m = dict(shared)
        m["x"] = np.ascontiguousarray(x[b])
        in_maps.append(m)
    res = run_bass_kernel_spmd(nc, in_maps, core_ids=list(range(ncores)))
    out = np.stack([np.asarray(res.results[b]["y"], dtype=np.float32) for b in range(ncores)], axis=0)
    return out
```

```python
import numpy as np
import concourse.bass as bass
import concourse.mybir as mybir
from concourse.bass_utils import run_bass_kernel_spmd

F32 = mybir.dt.float32
BF16 = mybir.dt.bfloat16
AF = mybir.ActivationFunctionType
ALU = mybir.AluOpType
AX = mybir.AxisListType

ENGINES = ("sync", "scalar", "gpsimd", "vector", "tensor")
EPOCH = 12000

S = 4096
D = 1024
NCH = 32
C = 128
DEPTH = 2
ALPHA = (2 * DEPTH) ** 0.25
LN_EPS = 1e-5
GN_EPS = 1e-5
FD = 2816
FE = 1408
NE = 8
TG = 1024
NTG = S // TG
TPG = TG // 128


class Buf:
    __slots__ = ("name", "last_w", "readers", "psum")

    def __init__(self, name):
        self.name = name
        self.last_w = None
        self.readers = []
        self.psum = False


class Op:
    __slots__ = ("eng", "fn", "deps", "is_dma", "ndma", "key", "signal", "sem_key", "sem_val")

    def __init__(self, eng, fn, is_dma, ndma, key):
        self.eng = eng
        self.fn = fn
        self.deps = []
        self.is_dma = is_dma
        self.ndma = ndma
        self.key = key
        self.signal = False
        self.sem_key = None
        self.sem_val = None


class Prog:
    def __init__(self, nc):
        self.nc = nc
        self.ops = {e: [] for e in ENGINES}
        self.all_ops = []

    def _add(self, op, reads, writes):
        deps = []
        for b in reads:
            if b.last_w is not None:
                deps.append(b.last_w)
            if b.psum:
                deps.extend(r for r in b.readers if r.eng != op.eng)
        for b in writes:
            if b.last_w is not None:
                deps.append(b.last_w)
            deps.extend(b.readers)
        out = []
        seen = set()
        for d in deps:
            if d is op or id(d) in seen:
                continue
            seen.add(id(d))
            if (not d.is_dma) and d.eng == op.eng and not op.is_dma:
                if op.eng == "tensor":
                    continue
            out.append(d)
        op.deps = out
        for b in reads:
            b.readers.append(op)
        for b in writes:
            b.last_w = op
            b.readers = []
        self.ops[op.eng].append(op)
        self.all_ops.append(op)
        return op

    def op(self, eng, fn, reads=(), writes=()):
        return self._add(Op(eng, fn, False, 0, None), list(reads), list(writes))

    def dma(self, eng, fn, reads=(), writes=(), key=None, ndma=1):
        if key is None:
            key = writes[0] if writes else reads[0]
        return self._add(Op(eng, fn, True, ndma, key), list(reads), list(writes))

    def fence(self):
        targets = []
        for e in ENGINES:
            for op in reversed(self.ops[e]):
                if not op.is_dma and op.fn is not None:
                    targets.append(op)
                    break
        lastdma = {}
        for op in self.all_ops:
            if op.is_dma:
                lastdma[id(op.key)] = op
        targets += list(lastdma.values())
        for e in ENGINES:
            f = Op(e, None, False, 0, None)
            f.deps = [t for t in targets if not (t.eng == e and e == "tensor" and not t.is_dma)]
            self.ops[e].append(f)
            self.all_ops.append(f)

    def emit(self):
        nc = self.nc
        for op in self.all_ops:
            for d in op.deps:
                d.signal = True
        sems = {}
        cnt = {}
        for e in ENGINES:
            n = 0
            for op in self.ops[e]:
                if op.is_dma:
                    k = ("dma", id(op.key))
                    cnt[k] = cnt.get(k, 0) + 16 * op.ndma
                    op.sem_key = k
                    op.sem_val = cnt[k]
                elif op.signal:
                    n += 1
                    ep, v = divmod(n - 1, EPOCH)
                    op.sem_key = (e, ep)
                    op.sem_val = v + 1
        for op in self.all_ops:
            if op.sem_key is not None and op.sem_key not in sems:
                sems[op.sem_key] = nc.alloc_semaphore("s%d" % len(sems))
        self.nsems = len(sems)

        def run_engine(e):
            def body(eng):
                known = {}
                for op in self.ops[e]:
                    need = {}
                    for d in op.deps:
                        k, v = d.sem_key, d.sem_val
                        if known.get(k, 0) >= v:
                            continue
                        if need.get(k, 0) < v:
                            need[k] = v
                    for k, v in need.items():
                        eng.wait_ge(sems[k], v)
                        known[k] = v
                    if op.fn is None:
                        continue
                    if op.is_dma:
                        op.fn(eng, sems[op.sem_key])
                    else:
                        inst = op.fn(eng)
                        if op.signal:
                            inst.then_inc(sems[op.sem_key], 1)
                last = {}
                for op in self.ops[e]:
                    if op.is_dma:
                        last[op.sem_key] = op.sem_val
                for k, v in last.items():
                    if known.get(k, 0) < v:
                        eng.wait_ge(sems[k], v)
            return body

        with nc.Block() as block:
            for e in ENGINES:
                if self.ops[e]:
                    getattr(block, e)(run_engine(e))


class T:
    __slots__ = ("t", "b")

    def __init__(self, t, name):
        self.t = t
        self.b = Buf(name)

    def __getitem__(self, k):
        return self.t[k]


def host_constants():
    k = np.arange(128, dtype=np.float32)[:, None]
    q = np.arange(128, dtype=np.float32)[None, :]
    cst = {}
    slopes = np.exp2(-8.0 * np.arange(1, 9, dtype=np.float32) / 8.0)
    am = np.zeros((128, 3, 8, 128), np.float32)
    for j in range(3):
        if j == 0:
            dist = q - k + 128.0
        elif j == 1:
            dist = np.abs(q - k)
        else:
            dist = k + 128.0 - q
        valid = (dist <= 128.0).astype(np.float32)
        for h in range(8):
            am[:, j, h, :] = np.exp(-slopes[h] * dist) * valid
    cst["c_amask"] = am.reshape(128, 3 * 8 * 128)
    sc = 32.0 ** -0.5
    ef = np.maximum(q - k, 0.0)
    mf = (q >= k).astype(np.float32) * sc
    eb = np.maximum(k - q, 0.0)
    mb = (k > q).astype(np.float32) * sc
    qp1 = np.broadcast_to(q + 1.0, (128, 128))
    qm = np.broadcast_to(128.0 - q, (128, 128))
    cst["c_ret"] = np.ascontiguousarray(np.concatenate([ef, mf, eb, mb, qp1, qm], axis=1).astype(np.float32))
    kp = np.concatenate([127.0 - k, k], axis=1)
    cst["c_kpos"] = np.ascontiguousarray(kp.astype(np.float32))
    return cst


def build_program(stop="full"):
    nc = bass.Bass("TRN2", target_bir_lowering=False)
    P = Prog(nc)
    import os
    DBG = set(os.environ.get("K_DBG", "").split(","))

    def din(name, shape, dt=F32):
        return nc.dram_tensor(name, list(shape), dt, kind="ExternalInput").ap()

    x_in = din("x", [S, D])
    y_out = nc.dram_tensor("y", [S, D], F32, kind="ExternalOutput").ap()
    w_in = din("w_in", [DEPTH, D, 4864])
    w_oa = din("w_o_attn", [DEPTH, 512, D])
    w_or = din("w_o_ret", [DEPTH, 512, D])
    w_out = din("w_out", [DEPTH, D, D])
    ffn_w1 = din("ffn_w1", [1, D, FD])
    ffn_w3 = din("ffn_w3", [1, D, FD])
    ffn_w2 = din("ffn_w2", [1, FD, D])
    router_w = din("router_w", [1, D, NE])
    moe_w1 = din("moe_w1", [1, NE, D, FE])
    moe_w3 = din("moe_w3", [1, NE, D, FE])
    moe_w2 = din("moe_w2", [1, NE, FE, D])
    ln_tab = din("ln_tab", [10, 128, D])
    bgate = din("bgate", [DEPTH, 1, 2048])
    dec_bc = din("dec_bc", [128, DEPTH * 2 * 8])
    dec_pp = din("dec_pp", [128, DEPTH * 2 * 2])
    sink_bc = din("sink_bc", [128, DEPTH * 8])
    rb_bc = din("rb_bc", [128, NE])
    c_amask = din("c_amask", [128, 3 * 8 * 128])
    c_ret = din("c_ret", [128, 6 * 128])
    c_kpos = din("c_kpos", [128, 2])

    xs0 = nc.dram_tensor("xs0", [S, D], F32).ap()
    xs1 = nc.dram_tensor("xs1", [S, D], F32).ap()
    x1t = nc.dram_tensor("x1t", [NCH, 128, 1024], BF16).ap()
    art = nc.dram_tensor("art", [NCH, 128, 1024], BF16).ap()
    gts = nc.dram_tensor("gts", [S, 2048], BF16).ap()
    b_xs0 = [Buf("xs0_%d" % i) for i in range(NCH)]
    b_xs1 = [Buf("xs1_%d" % i) for i in range(NCH)]
    b_x1t = [Buf("x1t_%d" % i) for i in range(NCH)]
    b_art = [Buf("art_%d" % i) for i in range(NCH)]
    b_gts = [Buf("gts_%d" % i) for i in range(NCH)]
    b_y = [Buf("y%d" % i) for i in range(NCH)]

    cur = {"ph": None, "lo": None, "hi": None}
    ptrs = {}

    def sb(name, shape, dt):
        ph = cur["ph"]
        if ph is None:
            assert cur["lo"] is None
            return T(nc.alloc_sbuf_tensor(name, list(shape), dt), name)
        size = int(np.prod(shape[1:])) * mybir.dt.size(dt)
        size = (size + 31) // 32 * 32
        off = ptrs.get(ph, cur["lo"])
        assert off + size <= cur["hi"], "SBUF arena overflow in phase %s at %s (%d > %d)" % (ph, name, off + size, cur["hi"])
        ptrs[ph] = off + size
        return T(nc.alloc_sbuf_tensor_at(name, list(shape), dt, offset=off), name)

    def open_arena():
        lo, hi = nc.bump_sbuf(nc.sbuf_bytes_remaining - 64)
        cur["lo"], cur["hi"] = lo, hi

    ps_f = [T(nc.alloc_psum_tensor("psf%d" % i, [128, 512], F32), "psf%d" % i) for i in range(6)]
    ps_b = [T(nc.alloc_psum_tensor("psb%d" % i, [128, 1024], BF16), "psb%d" % i) for i in range(2)]
    for t_ in ps_f + ps_b:
        t_.b.psum = True
    ring = {"f": 0, "b": 0}

    def bank():
        t = ps_f[ring["f"] % 6]
        ring["f"] += 1
        return t

    def tbank():
        t = ps_b[ring["b"] % 2]
        ring["b"] += 1
        return t

    def mm(out, lhsT, rhs, start, stop, reads, writes, tp=None):
        if tp is None:
            P.op("tensor", lambda e: e.matmul(out, lhsT, rhs, start=start, stop=stop), reads, writes)
        else:
            P.op("tensor", lambda e: e.matmul(out, lhsT, rhs, start=start, stop=stop, tile_position=tp),
                 reads, writes)

    def tr(out, in_, reads, writes):
        P.op("tensor", lambda e: e.transpose(out, in_, ident[:]), list(reads) + [ident.b], writes)

    def act(out, in_, func, reads, writes, scale=1.0, bias=0.0):
        P.op("scalar", lambda e: e.activation(out, in_, func, bias=bias, scale=scale), reads, writes)

    def vcopy(eng, out, in_, reads, writes):
        P.op(eng, lambda e: e.tensor_copy(out, in_), reads, writes)

    def tt(eng, out, a, b, op, reads, writes):
        P.op(eng, lambda e: e.tensor_tensor(out, a, b, op), reads, writes)

    def ts(eng, out, a, s1, s2, op0, op1, reads, writes):
        if s2 is None:
            P.op(eng, lambda e: e.tensor_scalar(out, a, s1, None, op0), reads, writes)
        else:
            P.op(eng, lambda e: e.tensor_scalar(out, a, s1, s2, op0, op1), reads, writes)

    def stt(eng, out, in0, scalar, in1, op0, op1, reads, writes):
        P.op(eng, lambda e: e.scalar_tensor_tensor(out, in0, scalar, in1, op0, op1), reads, writes)

    def load(eng, out_ap, in_ap, writes, reads=(), key=None):
        P.dma(eng, lambda e, s: e.dma_start(out=out_ap, in_=in_ap).then_inc(s, 16), reads, writes, key=key)

    def store(eng, out_ap, in_ap, reads, writes, key=None):
        P.dma(eng, lambda e, s: e.dma_start(out=out_ap, in_=in_ap).then_inc(s, 16), reads, writes, key=key)

    ident = sb("ident", [128, 128], BF16)
    identf = sb("identf", [128, 128], F32)
    ones1 = sb("ones1", [1, 128], BF16)
    P.op("vector", lambda e: e.memset(identf[:], 1.0), [], [identf.b])
    P.op("gpsimd", lambda e: e.affine_select(out=identf[:], in_=identf[:], pattern=[[-1, 128]],
                                             compare_op=ALU.is_equal, fill=0.0, base=0, channel_multiplier=1),
         [identf.b], [identf.b])
    vcopy("vector", ident[:], identf[:], [identf.b], [ident.b])
    P.op("vector", lambda e: e.memset(ones1[:], 1.0), [], [ones1.b])

    lnp = sb("lnp", [128, 2, D], F32)

    def load_ln(idx):
        load("sync", lnp[:, 0, :], ln_tab[idx], [lnp.b])
        load("sync", lnp[:, 1, :], ln_tab[idx + 1], [lnp.b])

    NLN = 8
    ln_sts = [sb("ln_st%d" % i, [128, 2, 6], F32) for i in range(NLN)]
    ln_mvs = [sb("ln_mv%d" % i, [128, 2], F32) for i in range(NLN)]
    ln_rss = [sb("ln_rs%d" % i, [128, 1], F32) for i in range(NLN)]
    ln_nbs = [sb("ln_nb%d" % i, [128, 1], F32) for i in range(NLN)]
    ln_i = {"i": 0}
    epsln = sb("epsln", [128, 1], F32)
    mhalf = sb("mhalf", [128, 8], F32)
    mhalf16 = sb("mhalf16", [128, 16], F32)
    P.op("vector", lambda e: e.memset(epsln[:], LN_EPS), [], [epsln.b])
    P.op("vector", lambda e: e.memset(mhalf[:], -0.5), [], [mhalf.b])
    P.op("vector", lambda e: e.memset(mhalf16[:], -0.5), [], [mhalf16.b])

    def ln_stages(src, dst, rd, wr, beta_eng="gpsimd", split_pow=False):
        k_ = ln_i["i"] % NLN
        ln_i["i"] += 1
        ln_st, ln_mv, ln_rs, ln_nb = ln_sts[k_], ln_mvs[k_], ln_rss[k_], ln_nbs[k_]

        def s1():
            P.op("vector", lambda e: e.bn_stats(ln_st[:, 0, :], src[:, 0:512]), rd, [ln_st.b])
            P.op("vector", lambda e: e.bn_stats(ln_st[:, 1, :], src[:, 512:1024]), rd, [ln_st.b])
            P.op("vector", lambda e: e.bn_aggr(ln_mv[:], ln_st[:]), [ln_st.b], [ln_mv.b])
            ts("vector", ln_rs[:], ln_mv[:, 1:2], LN_EPS, None, ALU.add, None, [ln_mv.b], [ln_rs.b])
            if not split_pow:
                s1b()

        def s1b():
            tt("gpsimd", ln_rs[:], ln_rs[:], mhalf[:, 0:1], ALU.pow, [ln_rs.b, mhalf.b], [ln_rs.b])

        def s2():
            stt("vector", ln_nb[:], ln_mv[:, 0:1], -1.0, ln_rs[:], ALU.mult, ALU.mult, [ln_mv.b, ln_rs.b], [ln_nb.b])
            P.op("scalar", lambda e: e.activation(dst, src, AF.Identity, bias=ln_nb[:], scale=ln_rs[:]),
                 list(rd) + [ln_rs.b, ln_nb.b], wr)

        def s3():
            tt("vector", dst, dst, lnp[:, 0, :], ALU.mult, list(wr) + [lnp.b], wr)
            tt(beta_eng, dst, dst, lnp[:, 1, :], ALU.add, list(wr) + [lnp.b], wr)

        if split_pow:
            return s1, s1b, s2, s3
        return s1, s2, s3

    def layer_norm(src, dst, rd, wr, beta_eng="gpsimd"):
        for f in ln_stages(src, dst, rd, wr, beta_eng):
            f()

    amask = sb("amask", [128, 3, 8, 128], BF16)
    cret = sb("cret", [128, 6, 128], F32)
    ckpos = sb("ckpos", [128, 2], F32)
    decb = sb("decb", [128, DEPTH * 2 * 8], F32)
    decp = sb("decp", [128, DEPTH * 2 * 2], F32)
    lgb = sb("lgb", [128, DEPTH * 2 * 8], F32)
    lgp = sb("lgp", [128, DEPTH * 2 * 2], F32)
    sinkb = sb("sinkb", [128, DEPTH * 8], F32)
    sinke = sb("sinke", [128, DEPTH * 8], F32)
    open_arena()

    cur["ph"] = "P0"
    FUSE_P0 = stop != "x0"
    load_ln(0)
    p0_in = [sb("p0in%d" % i, [128, D], F32) for i in range(4)]
    p0_out = [sb("p0out%d" % i, [128, D], F32) for i in range(4)]
    for n in range(0 if FUSE_P0 else 2):
        load("sync", p0_in[n % 4][:], x_in[n * 128:(n + 1) * 128, :], [p0_in[n % 4].b])
    for n in range(0 if FUSE_P0 else NCH):
        xi, xo = p0_in[n % 4], p0_out[n % 4]
        if n + 2 < NCH:
            load("sync", p0_in[(n + 2) % 4][:], x_in[(n + 2) * 128:(n + 3) * 128, :], [p0_in[(n + 2) % 4].b])
        layer_norm(xi[:], xo[:], [xi.b], [xo.b])
        dst = y_out if stop == "x0" else xs0
        store("sync", dst[n * 128:(n + 1) * 128, :], xo[:], [xo.b], [b_xs0[n]], key=xo.b)
    if stop == "x0":
        P.emit()
        return nc

    cur["ph"] = "MIX"
    Wtm = sb("Wtm", [128, 8, 1920], BF16)
    WtmA_b = Buf("WtmA")
    WtmB_b = Buf("WtmB")
    Wg = sb("Wg", [128, 8, 2048], BF16)
    Wfm = sb("Wfm", [128, 8, 1152], BF16)
    bgr = sb("bgr", [1, 2048], BF16)
    vtmp = [sb("vtmp%d" % i, [128, 512], F32) for i in range(2)]
    vmean = sb("vmean", [128, 64], F32)
    maskF = sb("maskF", [128, 8, 128], BF16)
    maskB = sb("maskB", [128, 8, 128], BF16)
    mtmp = sb("mtmp", [128, 128], F32)
    QF = sb("QF", [128, 2, 128], F32)
    QB = sb("QB", [128, 2, 128], F32)
    KF = sb("KF", [128, 8], F32)
    KBt = sb("KBt", [128, 8], F32)
    GCF = sb("GCF", [128, 2], F32)
    GCB = sb("GCB", [128, 2], F32)
    prevB = sb("prevB", [128, NCH, 2, 64], BF16)
    Sf = sb("Sf", [128, 2, 64], F32)
    Sb_ = sb("Sb", [128, 2, 64], F32)
    SfBF = sb("SfBF", [128, 2, 64], BF16)

    load("gpsimd", amask[:].rearrange("p j h q -> p (j h q)"), c_amask, [amask.b])
    load("sync", cret[:].rearrange("p a q -> p (a q)"), c_ret, [cret.b])
    load("sync", ckpos[:], c_kpos, [ckpos.b])
    load("sync", decb[:], dec_bc, [decb.b])
    load("sync", decp[:], dec_pp, [decp.b])
    load("sync", sinkb[:], sink_bc, [sinkb.b])
    for src_, dst_ in ((decb, lgb), (decp, lgp)):
        act(dst_[:], src_[:], AF.Exp, [src_.b], [dst_.b], scale=-1.0)
        act(dst_[:], dst_[:], AF.Ln, [dst_.b], [dst_.b], bias=1.0)
        ts("vector", dst_[:], dst_[:], -1.0, None, ALU.mult, None, [dst_.b], [dst_.b])
    act(sinke[:], sinkb[:], AF.Exp, [sinkb.b], [sinke.b])

    SC = 32.0 ** -0.5

    def mixer_setup(l):
        P.fence()
        for v in Vx:
            P.op("vector", lambda e, v=v: e.memset(v[:], 1.0), [], [v.b])
        wl = w_in[l].rearrange("(c p) n -> p c n", p=128)
        P.dma("gpsimd", lambda e, s: (
            e.dma_start(out=Wtm[:, :, 0:128], in_=wl[:, :, 640:768]).then_inc(s, 16),
            e.dma_start(out=Wtm[:, :, 128:384], in_=wl[:, :, 1024:1280]).then_inc(s, 16)),
            [], [WtmA_b], ndma=2)
        P.dma("gpsimd", lambda e, s: e.dma_start(out=Wtm[:, :, 896:1920], in_=wl[:, :, 1792:2816]).then_inc(s, 16),
              [], [WtmB_b])
        load("gpsimd", Wg[:], wl[:, :, 2816:4864], [Wg.b])

        def wfm_loads(e, s):
            for g in range(2):
                for c in range(8):
                    e.dma_start(
                        out=Wfm[:, c, 0:512].rearrange("p (r g2 e) -> p r g2 e", r=4, g2=2)[:, :, g, :],
                        in_=wl[:, c, g * 256:(g + 1) * 256].rearrange("p (r e) -> p r e", r=4)).then_inc(s, 16)
            e.dma_start(out=Wfm[:, :, 512:640], in_=wl[:, :, 512:640]).then_inc(s, 16)
            e.dma_start(out=Wfm[:, :, 640:1152], in_=wl[:, :, 768:1280]).then_inc(s, 16)
        P.dma("gpsimd", wfm_loads, [], [Wfm.b], ndma=18)
        load("gpsimd", bgr[:], bgate[l], [bgr.b])
        for c in range(8):
            vt = vtmp[c % 2]
            load("sync", vt[:], wl[:, c, 1280:1792], [vt.b])
            P.op("vector", lambda e, vt=vt, c=c: e.reduce_sum(vmean[:, c * 8:(c + 1) * 8],
                                                          vt[:].rearrange("p (h e) -> p h e", e=64), axis=AX.X),
                 [vt.b], [vmean.b])
            ts("vector", vmean[:, c * 8:(c + 1) * 8], vmean[:, c * 8:(c + 1) * 8], -1.0 / 64.0, None, ALU.mult, None,
               [vmean.b], [vmean.b])
            tt("vector", Wtm[:, c, 384:896].rearrange("p (h e) -> p h e", e=64),
               vt[:].rearrange("p (h e) -> p h e", e=64),
               vmean[:, c * 8:(c + 1) * 8].unsqueeze(2).to_broadcast([128, 8, 64]), ALU.add,
               [vt.b, vmean.b], [WtmA_b])
        of, ob = (l * 2 + 0) * 8, (l * 2 + 1) * 8
        for h in range(8):
            act(mtmp[:], cret[:, 0, :], AF.Exp, [cret.b, lgb.b], [mtmp.b], scale=lgb[:, of + h:of + h + 1])
            tt("vector", maskF[:, h, :], mtmp[:], cret[:, 1, :], ALU.mult, [mtmp.b, cret.b], [maskF.b])
            act(mtmp[:], cret[:, 2, :], AF.Exp, [cret.b, lgb.b], [mtmp.b], scale=lgb[:, ob + h:ob + h + 1])
            tt("vector", maskB[:, h, :], mtmp[:], cret[:, 3, :], ALU.mult, [mtmp.b, cret.b], [maskB.b])
        pf, pb = (l * 2 + 0) * 2, (l * 2 + 1) * 2
        for hh in range(2):
            act(QF[:, hh, :], cret[:, 4, :], AF.Exp, [cret.b, lgp.b], [QF.b], scale=lgp[:, pf + hh:pf + hh + 1])
            act(QB[:, hh, :], cret[:, 5, :], AF.Exp, [cret.b, lgp.b], [QB.b], scale=lgp[:, pb + hh:pb + hh + 1])
        act(KF[:], lgb[:, of:of + 8], AF.Exp, [lgb.b, ckpos.b], [KF.b], scale=ckpos[:, 0:1])
        ts("vector", KF[:], KF[:], SC, None, ALU.mult, None, [KF.b], [KF.b])
        act(KBt[:], lgb[:, ob:ob + 8], AF.Exp, [lgb.b, ckpos.b], [KBt.b], scale=ckpos[:, 1:2])
        ts("vector", KBt[:], KBt[:], SC, None, ALU.mult, None, [KBt.b], [KBt.b])
        act(GCF[:], lgp[:, pf:pf + 2], AF.Exp, [lgp.b], [GCF.b], scale=128.0)
        act(GCB[:], lgp[:, pb:pb + 2], AF.Exp, [lgp.b], [GCB.b], scale=128.0)

    xin = [sb("xin%d" % i, [128, D], F32) for i in range(3)]
    xbf = [sb("xbf%d" % i, [128, D], BF16) for i in range(3)]
    hT = [sb("hT%d" % i, [128, 8, 128], BF16) for i in range(3)]
    vr = [sb("vr%d" % i, [128, 512], BF16) for i in range(2)]
    kt = [sb("kt%d" % i, [128, 256], BF16) for i in range(2)]
    sgx = [sb("sgx%d" % i, [128, 2, 512], BF16) for i in range(2)]
    gsb = [sb("gsb%d" % i, [128, 2048], BF16) for i in range(2)]
    qaT = [sb("qaT%d" % i, [128, 4, 128], BF16) for i in range(2)]
    qrR = [sb("qrR%d" % i, [128, 2, 128], BF16) for i in range(2)]
    qrF = [sb("qrF%d" % i, [128, 2, 128], BF16) for i in range(2)]
    qrB = [sb("qrB%d" % i, [128, 2, 128], BF16) for i in range(2)]
    krT = [sb("krT%d" % i, [128, 2, 128], BF16) for i in range(2)]
    KTc = [sb("KTc%d" % i, [128, 128], BF16) for i in range(4)]
    Vx = [sb("Vx%d" % i, [128, 2, 65], BF16) for i in range(4)]
    Ebuf = [sb("Ebuf%d" % i, [128, 512], BF16) for i in range(3)]
    Pm = [sb("Pm%d" % i, [128, 4, 128], BF16) for i in range(6)]
    den = sb("den", [128, 8], F32)
    Abf = sb("Abf", [128, 8, 64], BF16)
    ARs = [sb("ARs%d" % i, [128, 8, 128], BF16) for i in range(2)]
    SFm = sb("SFm", [128, 8, 128], BF16)
    SBm = sb("SBm", [128, 8, 128], BF16)
    sq = sb("sq", [128, 2, 512], F32)
    ssq = sb("ssq", [128, 16], F32)
    sgr = sb("sgr", [128, 2, 512], F32)
    tfb = sb("tfb", [128, 2, 512], F32)
    ycp = sb("ycp", [128, 2, 512], F32)
    Rbf = sb("Rbf", [128, 512], BF16)

    def load_chunk_hT(n, src, bsrc, slot):
        xi, xb_, h = xin[slot], xbf[slot], hT[slot]
        load("sync", xi[:], src[n * 128:(n + 1) * 128, :], [xi.b], reads=[bsrc[n]])
        vcopy("vector", xb_[:], xi[:], [xi.b], [xb_.b])
        tb = tbank()
        for c in range(8):
            tr(tb[:, c * 128:(c + 1) * 128], xb_[:, c * 128:(c + 1) * 128], [xb_.b], [tb.b])
        vcopy("vector", h[:].rearrange("p c t -> p (c t)"), tb[:], [tb.b], [h.b])

    def proj_tm(h, W, lo, hi, bias=None):
        bk = bank()
        n = hi - lo
        wbuf = W.b
        if W is Wtm:
            wbuf = WtmA_b if hi <= 896 else WtmB_b
        for c in range(8):
            mm(bk[:, 0:n], h[:, c, :], W[:, c, lo:hi], c == 0, (c == 7 and bias is None), [h.b, wbuf], [bk.b])
        if bias is not None and "nobias" not in DBG:
            mm(bk[:, 0:n], ones1[:], bias, False, True, [ones1.b, bgr.b], [bk.b])
        return bk

    def state_update(St, GC, ktile, vtile):
        bk = bank()
        for h in range(8):
            hh, hl = divmod(h, 4)
            mm(bk[hl * 32:(hl + 1) * 32, hh * 64:(hh + 1) * 64], ktile[:, h * 32:(h + 1) * 32],
               vtile[:, h * 64:(h + 1) * 64], True, True, [ktile.b, vtile.b], [bk.b], tp=(0, hl * 32))
        for hh in range(2):
            stt("vector", St[:, hh, :], St[:, hh, :], GC[:, hh:hh + 1], bk[:, hh * 64:(hh + 1) * 64],
                ALU.mult, ALU.add, [St.b, GC.b, bk.b], [St.b])

    def mixer_M1(l, src, bsrc, fuse_ln=False):
        P.op("vector", lambda e: e.memset(Sb_[:], 0.0), [], [Sb_.b])
        order = list(range(NCH - 1, -1, -1))
        raw = x_in if fuse_ln else src

        def ld(n):
            xi = xin[n % 3]
            if fuse_ln:
                load("sync", xi[:], raw[n * 128:(n + 1) * 128, :], [xi.b])
            else:
                load("sync", xi[:], raw[n * 128:(n + 1) * 128, :], [xi.b], reads=[bsrc[n]])

        def prep_stages(n):
            xi, xb_ = xin[n % 3], xbf[n % 3]
            if fuse_ln:
                s1, s2, s3 = ln_stages(xi[:], xi[:], [xi.b], [xi.b])
            else:
                s1 = s2 = s3 = (lambda: None)

            def s3b():
                s3()
                if fuse_ln:
                    store("sync", src[n * 128:(n + 1) * 128, :], xi[:], [xi.b], [bsrc[n]], key=xi.b)
                vcopy("vector", xb_[:], xi[:], [xi.b], [xb_.b])
            return s1, s2, s3b

        for j in range(3):
            ld(order[j])
            for f in prep_stages(order[j]):
                f()
        ld(order[3])
        A_h2(l, order[0])
        for i, n in enumerate(order):
            slot = n % 2
            if i + 1 < NCH:
                A_h2(l, order[i + 1])
            st = prep_stages(order[i + 3]) if i + 3 < NCH else None
            if st:
                st[0]()
            h = hT[n % 3]
            bk_k = proj_tm(h, Wtm, 128, 384)
            bk_v = proj_tm(h, Wtm, 384, 896)
            tt("vector", kt[slot][:].rearrange("p (h d) -> p h d", d=32),
               bk_k[:, 0:256].rearrange("p (h d) -> p h d", d=32),
               KBt[:].unsqueeze(2).to_broadcast([128, 8, 32]), ALU.mult, [bk_k.b, KBt.b], [kt[slot].b])
            act(vr[slot][:], bk_v[:], AF.Copy, [bk_v.b], [vr[slot].b])
            if st:
                st[1]()
            vcopy("vector", prevB[:, n, :, :], Sb_[:], [Sb_.b], [prevB.b])
            state_update(Sb_, GCB, kt[slot], vr[slot])
            if st:
                st[2]()
            if i + 4 < NCH:
                ld(order[i + 4])

    def A_h1(l, n, src, bsrc):
        xi, xb_ = xin[n % 3], xbf[n % 3]
        load("sync", xi[:], src[n * 128:(n + 1) * 128, :], [xi.b], reads=[bsrc[n]])
        vcopy("vector", xb_[:], xi[:], [xi.b], [xb_.b])

    def A_h2(l, n):
        xb_, h = xbf[n % 3], hT[n % 3]
        tb = tbank()
        for c in range(8):
            tr(tb[:, c * 128:(c + 1) * 128], xb_[:, c * 128:(c + 1) * 128], [xb_.b], [tb.b])
        vcopy("vector", h[:].rearrange("p c t -> p (c t)"), tb[:], [tb.b], [h.b])

    def A_kv(l, n):
        slot = n % 2
        h = hT[n % 3]
        bk = bank()
        for c in range(8):
            mm(bk[:, 0:128], Wfm[:, c, 512:640], h[:, c, :], c == 0, c == 7, [Wfm.b, h.b], [bk.b])
        vcopy("vector", KTc[n % 4][:], bk[:, 0:128], [bk.b], [KTc[n % 4].b])
        bk = proj_tm(h, Wtm, 0, 384)
        vcopy("vector", Vx[n % 4][:, :, 0:64], bk[:, 0:128].rearrange("p (g e) -> p g e", e=64), [bk.b], [Vx[n % 4].b])
        tt("vector", kt[slot][:].rearrange("p (h d) -> p h d", d=32),
           bk[:, 128:384].rearrange("p (h d) -> p h d", d=32),
           KF[:].unsqueeze(2).to_broadcast([128, 8, 32]), ALU.mult, [bk.b, KF.b], [kt[slot].b])

    def A_rest(l, n):
        slot = n % 2
        h = hT[n % 3]
        bk = proj_tm(h, Wtm, 384, 896)
        act(vr[slot][:], bk[:], AF.Copy, [bk.b], [vr[slot].b])
        bk = proj_tm(h, Wtm, 896, 1408)
        act(sgx[slot][:, 0, :], bk[:], AF.Silu, [bk.b], [sgx[slot].b])
        bk = proj_tm(h, Wtm, 1408, 1920)
        act(sgx[slot][:, 1, :], bk[:], AF.Silu, [bk.b], [sgx[slot].b])
        for j in range(4):
            bk = proj_tm(h, Wg, j * 512, (j + 1) * 512, bias=bgr[:, j * 512:(j + 1) * 512])
            act(gsb[slot][:, j * 512:(j + 1) * 512], bk[:], AF.Sigmoid, [bk.b], [gsb[slot].b])
        store("sync", gts[n * 128:(n + 1) * 128, :], gsb[slot][:], [gsb[slot].b], [b_gts[n]], key=gsb[slot].b)
        bk = bank()
        for blk in range(4):
            for c in range(8):
                mm(bk[:, blk * 128:(blk + 1) * 128], Wfm[:, c, blk * 128:(blk + 1) * 128], h[:, c, :],
                   c == 0, c == 7, [Wfm.b, h.b], [bk.b])
        vcopy("vector", qaT[slot][:].rearrange("p r t -> p (r t)"), bk[:], [bk.b], [qaT[slot].b])
        bk = bank()
        for blk in range(4):
            for c in range(8):
                mm(bk[:, blk * 128:(blk + 1) * 128], Wfm[:, c, 640 + blk * 128:640 + (blk + 1) * 128], h[:, c, :],
                   c == 0, c == 7, [Wfm.b, h.b], [bk.b])
        act(qrR[slot][:].rearrange("p a t -> p (a t)"), bk[:, 0:256], AF.Copy, [bk.b], [qrR[slot].b])
        act(krT[slot][:].rearrange("p a t -> p (a t)"), bk[:, 256:512], AF.Copy, [bk.b], [krT[slot].b])
        tt("vector", qrF[slot][:].rearrange("p a t -> p (a t)"), bk[:, 0:256], QF[:].rearrange("p a t -> p (a t)"),
           ALU.mult, [bk.b, QF.b], [qrF[slot].b])
        tt("vector", qrB[slot][:].rearrange("p a t -> p (a t)"), bk[:, 0:256], QB[:].rearrange("p a t -> p (a t)"),
           ALU.mult, [bk.b, QB.b], [qrB[slot].b])

    pm_i = {"i": 0, "e": 0}
    pms_of = {}
    ybk_of = {}

    def S1(l, n):
        slot = n % 2
        for hl in range(4):
            bk = bank()
            for hh in range(2):
                mm(bk[:, hh * 128:(hh + 1) * 128], krT[slot][hl * 32:(hl + 1) * 32, hh, :],
                   qrR[slot][hl * 32:(hl + 1) * 32, hh, :], True, True, [krT[slot].b, qrR[slot].b], [bk.b],
                   tp=(hl * 32, 0))
            bv = bk[:, 0:256].rearrange("p (h t) -> p h t", t=128)
            tt("vector", SFm[:, hl:8:4, :], bv, maskF[:, hl:8:4, :], ALU.mult, [bk.b, maskF.b], [SFm.b])
            tt("vector", SBm[:, hl:8:4, :], bv, maskB[:, hl:8:4, :], ALU.mult, [bk.b, maskB.b], [SBm.b])
        js = [j for j in range(3) if 0 <= n - 1 + j < NCH]
        pms = {}
        for j in js:
            kc = KTc[(n - 1 + j) % 4]
            for g in range(2):
                bk = bank()
                mm(bk[:], kc[g * 64:(g + 1) * 64, :], qaT[slot][g * 64:(g + 1) * 64].rearrange("p r t -> p (r t)"),
                   True, True, [kc.b, qaT[slot].b], [bk.b], tp=(g * 64, 0))
                eb = Ebuf[pm_i["e"] % 3]
                pm_i["e"] += 1
                act(eb[:], bk[:], AF.Exp, [bk.b], [eb.b], scale=0.125)
                pm = Pm[pm_i["i"] % 6]
                pm_i["i"] += 1
                tt("gpsimd", pm[:], eb[:].rearrange("p (r t) -> p r t", t=128), amask[:, j, g * 4:(g + 1) * 4, :],
                   ALU.mult, [eb.b, amask.b], [pm.b])
                pms[(j, g)] = pm
        pms_of[n] = (js, pms)

    def S2(l, n):
        slot = n % 2
        js, pms = pms_of.pop(n)
        for hb in range(2):
            bk = bank()
            for r in range(4):
                for ji, j in enumerate(js):
                    vx = Vx[(n - 1 + j) % 4]
                    pm = pms[(j, hb)]
                    mm(bk[:, r * 65:(r + 1) * 65], pm[:, r, :], vx[:, hb, :], ji == 0, ji == len(js) - 1,
                       [pm.b, vx.b], [bk.b])
            ov = bk[:, 0:260].rearrange("p (r e) -> p r e", e=65)
            tt("vector", den[:, hb * 4:(hb + 1) * 4], ov[:, :, 64], sinke[:, l * 8 + hb * 4:l * 8 + (hb + 1) * 4],
               ALU.add, [bk.b, sinke.b], [den.b])
            P.op("vector", lambda e, hb=hb: e.reciprocal(den[:, hb * 4:(hb + 1) * 4], den[:, hb * 4:(hb + 1) * 4]),
                 [den.b], [den.b])
            tt("vector", Abf[:, hb * 4:(hb + 1) * 4, :], ov[:, :, 0:64],
               den[:, hb * 4:(hb + 1) * 4].unsqueeze(2).to_broadcast([128, 4, 64]), ALU.mult,
               [bk.b, den.b], [Abf.b])
        ybk = []
        for d, (Sm, qx, has_cross) in enumerate(((SFm, qrF[slot], n > 0), (SBm, qrB[slot], n < NCH - 1))):
            bk = bank()
            for h in range(8):
                hh, hl = divmod(h, 4)
                mm(bk[:, h * 64:(h + 1) * 64], Sm[:, h, :], vr[slot][:, h * 64:(h + 1) * 64], True, not has_cross,
                   [Sm.b, vr[slot].b], [bk.b])
                if has_cross:
                    if d == 0:
                        rhs, rb = SfBF[hl * 32:(hl + 1) * 32, hh, :], SfBF.b
                    else:
                        rhs, rb = prevB[hl * 32:(hl + 1) * 32, n, hh, :], prevB.b
                    mm(bk[:, h * 64:(h + 1) * 64], qx[hl * 32:(hl + 1) * 32, hh, :], rhs, False, True,
                       [qx.b, rb], [bk.b], tp=(hl * 32, 0))
            ybk.append(bk)
        if n < NCH - 1:
            state_update(Sf, GCF, kt[slot], vr[slot])
            vcopy("vector", SfBF[:], Sf[:], [Sf.b], [SfBF.b])
        for d in range(2):
            bk = ybk[d]
            act(ycp[:, d, :], bk[:], AF.Copy, [bk.b], [ycp.b])
            act(sq[:, d, :], bk[:], AF.Square, [bk.b], [sq.b])
        P.op("vector", lambda e: e.reduce_sum(ssq[:], sq[:].rearrange("p d (h e) -> p (d h) e", e=64), axis=AX.X),
             [sq.b], [ssq.b])
        ts("vector", ssq[:], ssq[:], 1.0 / 64.0, GN_EPS, ALU.mult, ALU.add, [ssq.b], [ssq.b])
        tt("gpsimd", ssq[:], ssq[:], mhalf16[:], ALU.pow, [ssq.b, mhalf16.b], [ssq.b])
        tt("gpsimd", sgr[:].rearrange("p d (h e) -> p (d h) e", e=64),
           sgx[slot][:].rearrange("p d (h e) -> p (d h) e", e=64),
           ssq[:].unsqueeze(2).to_broadcast([128, 16, 64]), ALU.mult, [sgx[slot].b, ssq.b], [sgr.b])

    def S2b(l, n):
        tt("vector", tfb[:].rearrange("p d f -> p (d f)"), ycp[:].rearrange("p d f -> p (d f)"),
           sgr[:].rearrange("p d f -> p (d f)"), ALU.mult, [ycp.b, sgr.b], [tfb.b])
        tt("gpsimd", Rbf[:], tfb[:, 0, :], tfb[:, 1, :], ALU.add, [tfb.b], [Rbf.b])

    def S3(l, n):
        ars = ARs[n % 2]
        tb = tbank()
        af = Abf[:].rearrange("p h e -> p (h e)")
        for c in range(4):
            tr(tb[:, c * 128:(c + 1) * 128], af[:, c * 128:(c + 1) * 128], [Abf.b], [tb.b])
        for c in range(4):
            tr(tb[:, 512 + c * 128:512 + (c + 1) * 128], Rbf[:, c * 128:(c + 1) * 128], [Rbf.b], [tb.b])
        vcopy("vector", ars[:].rearrange("p c t -> p (c t)"), tb[:], [tb.b], [ars.b])
        store("sync", art[n], ars[:].rearrange("p c t -> p (c t)"), [ars.b], [b_art[n]],
              key=ars.b)

    def mixer_M2(l, src, bsrc):
        P.op("vector", lambda e: e.memset(Sf[:], 0.0), [], [Sf.b])
        A_h1(l, 0, src, bsrc)
        A_h1(l, 1, src, bsrc)
        A_h1(l, 2, src, bsrc)
        A_h2(l, 0)
        A_h2(l, 1)
        A_kv(l, 0)
        A_rest(l, 0)
        for n in range(NCH):
            if n + 2 < NCH:
                A_h2(l, n + 2)
            if n >= 1:
                S2b(l, n - 1)
            if n + 1 < NCH:
                A_kv(l, n + 1)
            S1(l, n)
            if n >= 1:
                S3(l, n - 1)
            if n + 1 < NCH:
                A_rest(l, n + 1)
            if n + 3 < NCH:
                A_h1(l, n + 3, src, bsrc)
            S2(l, n)
        S2b(l, NCH - 1)
        S3(l, NCH - 1)

    cur["ph"] = "O"
    Woa = sb("Woa", [128, 4, D], BF16)
    Wor = sb("Wor", [128, 4, D], BF16)
    Wo = sb("Wo", [128, 8, D], BF16)
    NO = 4
    o_ar = [sb("o_ar%d" % i, [128, 8, 128], BF16) for i in range(NO)]
    o_g = [sb("o_g%d" % i, [128, 2048], BF16) for i in range(NO)]
    o_x = [sb("o_x%d" % i, [128, D], F32) for i in range(NO)]
    o_m1 = [sb("o_m1%d" % i, [128, 512], F32) for i in range(4)]
    o_m2 = [sb("o_m2%d" % i, [128, 512], F32) for i in range(4)]
    o_mb = [sb("o_mb%d" % i, [128, D], BF16) for i in range(NO)]
    o_mT = [sb("o_mT%d" % i, [128, 8, 128], BF16) for i in range(NO)]
    NO2 = 6
    o_r = [sb("o_r%d" % i, [128, D], F32) for i in range(NO2)]
    o_x1 = [sb("o_x1%d" % i, [128, D], F32) for i in range(NO2)]
    o_x1b = [sb("o_x1b%d" % i, [128, D], BF16) for i in range(NO2)]
    o_x1T = [sb("o_x1T%d" % i, [128, 8, 128], BF16) for i in range(NO2)]
    om_i = {"i": 0}

    def phase_O(l, src, bsrc):
        P.fence()
        load("gpsimd", Woa[:], w_oa[l].rearrange("(c p) n -> p c n", p=128), [Woa.b])
        load("gpsimd", Wor[:], w_or[l].rearrange("(c p) n -> p c n", p=128), [Wor.b])
        load("gpsimd", Wo[:], w_out[l].rearrange("(c p) n -> p c n", p=128), [Wo.b])
        load_ln(2 + 2 * l)

        def OL(t):
            k = t % NO
            ar, g, xi = o_ar[k], o_g[k], o_x[k]
            load("sync", ar[:].rearrange("p c t -> p (c t)"), art[t], [ar.b], reads=[b_art[t]])
            load("sync", g[:], gts[t * 128:(t + 1) * 128, :], [g.b], reads=[b_gts[t]])
            load("sync", xi[:], src[t * 128:(t + 1) * 128, :], [xi.b], reads=[bsrc[t]])

        def OY(t):
            k = t % NO
            ar, g, xi = o_ar[k], o_g[k], o_x[k]
            for cb in range(2):
                bka = bank()
                for c in range(4):
                    mm(bka[:], ar[:, c, :], Woa[:, c, cb * 512:(cb + 1) * 512], c == 0, c == 3, [ar.b, Woa.b], [bka.b])
                bkr = bank()
                for c in range(4):
                    mm(bkr[:], ar[:, 4 + c, :], Wor[:, c, cb * 512:(cb + 1) * 512], c == 0, c == 3, [ar.b, Wor.b], [bkr.b])
                m1, m2 = o_m1[om_i["i"] % 4], o_m2[om_i["i"] % 4]
                om_i["i"] += 1
                tt("vector", m1[:], bka[:], g[:, cb * 512:(cb + 1) * 512], ALU.mult, [bka.b, g.b], [m1.b])
                tt("vector", m2[:], bkr[:], g[:, 1024 + cb * 512:1024 + (cb + 1) * 512], ALU.mult, [bkr.b, g.b], [m2.b])
                tt("gpsimd", o_mb[k][:, cb * 512:(cb + 1) * 512], m1[:], m2[:], ALU.add, [m1.b, m2.b], [o_mb[k].b])

        ln_of = {}

        def OT(t):
            k = t % NO
            k2 = t % NO2
            xi = o_x[k]
            tb = tbank()
            for c in range(8):
                tr(tb[:, c * 128:(c + 1) * 128], o_mb[k][:, c * 128:(c + 1) * 128], [o_mb[k].b], [tb.b])
            act(o_mT[k][:].rearrange("p c t -> p (c t)"), tb[:], AF.Copy, [tb.b], [o_mT[k].b])
            for cb in range(2):
                bk = bank()
                for c in range(8):
                    mm(bk[:], o_mT[k][:, c, :], Wo[:, c, cb * 512:(cb + 1) * 512], c == 0, c == 7, [o_mT[k].b, Wo.b], [bk.b])
                stt("vector", o_r[k2][:, cb * 512:(cb + 1) * 512], xi[:, cb * 512:(cb + 1) * 512], ALPHA, bk[:],
                    ALU.mult, ALU.add, [xi.b, bk.b], [o_r[k2].b])
            x1 = o_x1[k2]
            ln_of[t] = ln_stages(o_r[k2][:], x1[:], [o_r[k2].b], [x1.b], beta_eng="vector")
            ln_of[t][0]()

        def OL2(t):
            ln_of[t][1]()

        def OL3(t):
            k2 = t % NO2
            x1 = o_x1[k2]
            ln_of.pop(t)[2]()
            dst = y_out if stop == "l%d_x1" % l else xs1
            store("sync", dst[t * 128:(t + 1) * 128, :], x1[:], [x1.b], [b_xs1[t]], key=x1.b)
            act(o_x1b[k2][:], x1[:], AF.Copy, [x1.b], [o_x1b[k2].b])

        def OX(t):
            k2 = t % NO2
            tb = tbank()
            for c in range(8):
                tr(tb[:, c * 128:(c + 1) * 128], o_x1b[k2][:, c * 128:(c + 1) * 128], [o_x1b[k2].b], [tb.b])
            xT = o_x1T[k2]
            vcopy("vector", xT[:].rearrange("p c t -> p (c t)"), tb[:], [tb.b], [xT.b])
            store("sync", x1t[t], xT[:].rearrange("p c t -> p (c t)"), [xT.b], [b_x1t[t]],
                  key=xT.b)

        OL(0)
        OL(1)
        for i in range(NCH + 5):
            if i + 2 < NCH:
                OL(i + 2)
            if i < NCH:
                OY(i)
            if 0 <= i - 1 < NCH:
                OT(i - 1)
            if 0 <= i - 2 < NCH:
                OL2(i - 2)
            if 0 <= i - 3 < NCH:
                OL3(i - 3)
            if 0 <= i - 4 < NCH:
                OX(i - 4)

    cur["ph"] = "FF"
    X1T = [sb("X1T%d" % i, [128, TPG, 8, 128], BF16) for i in range(2)]
    XTb = [[Buf("XTb%d_%d" % (i, t)) for t in range(TPG)] for i in range(2)]
    ACC = sb("ACC", [128, TPG, D], F32)
    ACCb = [Buf("ACC%d" % t) for t in range(TPG)]
    hTb = [sb("hTb%d" % i, [128, 11, TG], BF16) for i in range(2)]
    WA = [sb("WA%d" % i, [128, 8, 2, 128], BF16) for i in range(3)]
    WB = [sb("WB%d" % i, [128, 11, D], BF16) for i in range(2)]
    stmp = [sb("stmp%d" % i, [128, 512], BF16) for i in range(2)]
    gate = [sb("gate%d" % i, [128, TPG, NE], F32) for i in range(2)]
    f_x2 = [sb("f_x2%d" % i, [128, D], F32) for i in range(2)]
    Wr = sb("Wr", [128, 8, NE], F32)
    rbb = sb("rbb", [128, NE], F32)
    r_x = [sb("r_x%d" % i, [128, D], F32) for i in range(2)]
    r_xT = sb("r_xT", [128, 8, 128], F32)
    r_lg = sb("r_lg", [128, NE], F32)
    r_m1 = sb("r_m1", [128, 1], F32)
    r_m2 = sb("r_m2", [128, 1], F32)
    r_k1 = sb("r_k1", [128, NE], F32)
    r_k2 = sb("r_k2", [128, NE], F32)
    r_l2 = sb("r_l2", [128, NE], F32)
    r_d = sb("r_d", [128, 1], F32)
    r_w1 = sb("r_w1", [128, 1], F32)
    r_w2 = sb("r_w2", [128, 1], F32)
    wa_i = {"i": 0, "b": 0, "h": 0, "s": 0}

    def phase_FF(l, is_moe, dst, bdst):
        P.fence()
        load_ln(6 + 2 * l)
        if is_moe:
            load("sync", Wr[:], router_w[0].rearrange("(c p) e -> p c e", p=128), [Wr.b])
            load("sync", rbb[:], rb_bc, [rbb.b])
            experts = [(moe_w1[0, e], moe_w3[0, e], moe_w2[0, e], 0, e) for e in range(NE)]
        else:
            experts = [(ffn_w1[0], ffn_w3[0], ffn_w2[0], hf * FE, None) for hf in range(2)]

        def ff_inputs(G):
            t0 = G * TPG
            X, gt = X1T[G % 2], gate[G % 2]
            for t in range(TPG):
                load("sync", X[:, t, :, :].rearrange("p c t -> p (c t)"), x1t[t0 + t], [XTb[G % 2][t]],
                     reads=[b_x1t[t0 + t]])
            if not is_moe:
                return
            for t in range(TPG):
                rx = r_x[t % 2]
                load("sync", rx[:], xs1[(t0 + t) * 128:(t0 + t + 1) * 128, :], [rx.b], reads=[b_xs1[t0 + t]])
                for half in range(2):
                    bk = bank()
                    for c in range(4):
                        cc = half * 4 + c
                        P.op("tensor", lambda e, bk=bk, c=c, cc=cc, rx=rx: e.transpose(
                            bk[:, c * 128:(c + 1) * 128], rx[:, cc * 128:(cc + 1) * 128], identf[:]),
                            [rx.b, identf.b], [bk.b])
                    vcopy("vector", r_xT[:, half * 4:(half + 1) * 4, :].rearrange("p c t -> p (c t)"), bk[:],
                          [bk.b], [r_xT.b])
                bk = bank()
                for c in range(8):
                    mm(bk[:, 0:NE], r_xT[:, c, :], Wr[:, c, :], c == 0, c == 7, [r_xT.b, Wr.b], [bk.b])
                tt("vector", r_lg[:], bk[:, 0:NE], rbb[:], ALU.add, [bk.b, rbb.b], [r_lg.b])
                P.op("vector", lambda e: e.reduce_max(r_m1[:], r_lg[:], axis=AX.X), [r_lg.b], [r_m1.b])
                ts("vector", r_k1[:], r_lg[:], r_m1[:], None, ALU.is_equal, None, [r_lg.b, r_m1.b], [r_k1.b])
                stt("vector", r_l2[:], r_k1[:], -1e30, r_lg[:], ALU.mult, ALU.add, [r_k1.b, r_lg.b], [r_l2.b])
                P.op("vector", lambda e: e.reduce_max(r_m2[:], r_l2[:], axis=AX.X), [r_l2.b], [r_m2.b])
                ts("vector", r_k2[:], r_l2[:], r_m2[:], None, ALU.is_equal, None, [r_l2.b, r_m2.b], [r_k2.b])
                tt("vector", r_d[:], r_m1[:], r_m2[:], ALU.subtract, [r_m1.b, r_m2.b], [r_d.b])
                act(r_w1[:], r_d[:], AF.Sigmoid, [r_d.b], [r_w1.b])
                act(r_w2[:], r_d[:], AF.Sigmoid, [r_d.b], [r_w2.b], scale=-1.0)
                ts("vector", r_k1[:], r_k1[:], r_w1[:], None, ALU.mult, None, [r_k1.b, r_w1.b], [r_k1.b])
                stt("vector", gt[:, t, :], r_k2[:], r_w2[:], r_k1[:], ALU.mult, ALU.add,
                    [r_k2.b, r_w2.b, r_k1.b], [gt.b])

        def acc_init_tile(G, t):
            t0 = G * TPG
            ax = r_x[t % 2]
            load("sync", ax[:], xs1[(t0 + t) * 128:(t0 + t + 1) * 128, :], [ax.b], reads=[b_xs1[t0 + t]])
            act(ACC[:, t, :], ax[:], AF.Identity, [ax.b], [ACCb[t]], scale=ALPHA)

        def epilogue_tile(G, t):
            for f in epilogue_stages(G, t):
                f()

        def epilogue_stages(G, t):
            t0 = G * TPG
            x2 = f_x2[t % 2]
            s1, s1b, s2, s3 = ln_stages(ACC[:, t, :], x2[:], [ACCb[t]], [x2.b], beta_eng="vector", split_pow=True)

            def s2b():
                s1b()
                s2()

            def s3b():
                s3()
                store("sync", dst[(t0 + t) * 128:(t0 + t + 1) * 128, :], x2[:], [x2.b], [bdst[t0 + t]], key=x2.b)
            return s1, s2b, s3b

        def expert(G, w1, w3, w2, foff, eidx, hooks=None):
            X, gt, xb_ = X1T[G % 2], gate[G % 2], XTb[G % 2]
            hb_ = hTb[wa_i["h"] % 2]
            wa_i["h"] += 1
            wb = WB[wa_i["b"] % 2]
            wa_i["b"] += 1
            load("gpsimd", wb[:], w2[foff:foff + FE, :].rearrange("(c p) n -> p c n", p=128), [wb.b])
            w1v = w1.rearrange("(c p) f -> p c f", p=128)
            w3v = w3.rearrange("(c p) f -> p c f", p=128)
            for fb in range(11):
                wa = WA[wa_i["i"] % 3]
                wa_i["i"] += 1
                f0 = foff + fb * 128
                P.dma("gpsimd", lambda e, s, wa=wa, f0=f0, w1v=w1v, w3v=w3v: (
                    e.dma_start(out=wa[:, :, 0, :], in_=w1v[:, :, f0:f0 + 128]).then_inc(s, 16),
                    e.dma_start(out=wa[:, :, 1, :], in_=w3v[:, :, f0:f0 + 128]).then_inc(s, 16)),
                    [], [wa.b], ndma=2)
                for tg in range(TG // 512):
                    xr = X[:, tg * 4:(tg + 1) * 4, :, :]
                    xbs = xb_[tg * 4:(tg + 1) * 4]
                    b1 = bank()
                    for c in range(8):
                        mm(b1[:].rearrange("p (a t) -> p a t", t=128), wa[:, c, 0, :], xr[:, :, c, :], c == 0, c == 7,
                           [wa.b] + xbs, [b1.b])
                    b3 = bank()
                    for c in range(8):
                        mm(b3[:].rearrange("p (a t) -> p a t", t=128), wa[:, c, 1, :], xr[:, :, c, :], c == 0, c == 7,
                           [wa.b] + xbs, [b3.b])
                    st = stmp[wa_i["s"] % 2]
                    wa_i["s"] += 1
                    act(st[:], b1[:], AF.Silu, [b1.b], [st.b])
                    tt("vector", hb_[:, fb, tg * 512:(tg + 1) * 512], st[:], b3[:], ALU.mult, [st.b, b3.b], [hb_.b])
                if hooks is not None:
                    for fn in hooks.get(fb, ()):
                        fn()
            for t in range(TPG):
                for cb in range(2):
                    bk = bank()
                    for fc in range(11):
                        mm(bk[:], hb_[:, fc, t * 128:(t + 1) * 128], wb[:, fc, cb * 512:(cb + 1) * 512],
                           fc == 0, fc == 10, [hb_.b, wb.b], [bk.b])
                    acc = ACC[:, t, cb * 512:(cb + 1) * 512]
                    if eidx is None:
                        tt("vector", acc, bk[:], acc, ALU.add, [bk.b, ACCb[t]], [ACCb[t]])
                    else:
                        stt("vector", acc, bk[:], gt[:, t, eidx:eidx + 1], acc, ALU.mult, ALU.add,
                            [bk.b, gt.b, ACCb[t]], [ACCb[t]])

        ff_inputs(0)
        for t in range(TPG):
            acc_init_tile(0, t)
        for G in range(NTG):
            for ei, (w1, w3, w2, foff, eidx) in enumerate(experts):
                hooks = None
                if ei == 0 and G > 0:
                    hooks = {}
                    for t in range(TPG):
                        e1, e2, e3 = epilogue_stages(G - 1, t)
                        hooks.setdefault(t, []).append(e1)
                        hooks.setdefault(t + 1, []).append(e2)
                        hooks.setdefault(t + 2, []).append(e3)
                        hooks.setdefault(t + 3, []).append(lambda G=G, t=t: acc_init_tile(G, t))
                expert(G, w1, w3, w2, foff, eidx, hooks)
                if ei == 0 and G + 1 < NTG:
                    ff_inputs(G + 1)
        for t in range(TPG):
            epilogue_tile(NTG - 1, t)

    for l in range(DEPTH):
        mixer_setup(l)
        if stop == "dbg_setup":
            break
        mixer_M1(l, xs0, b_xs0, fuse_ln=(l == 0))
        if stop == "dbg_m1":
            break
        mixer_M2(l, xs0, b_xs0)
        if stop == "dbg_m2":
            break
        phase_O(l, xs0, b_xs0)
        if stop == "l%d_x1" % l:
            break
        last = (l == DEPTH - 1) or stop == "l%d_x2" % l
        phase_FF(l, l % 2 == 1, y_out if last else xs0, b_y if last else b_xs0)
        if last:
            break
    P.emit()
    if os.environ.get("K_VERBOSE"):
        print("arena", cur["lo"], cur["hi"], {k: v - cur["lo"] for k, v in ptrs.items()}, "sems", P.nsems,
              {e: len(P.ops[e]) for e in ENGINES})
    return nc


_CACHE = {}


def _host_inputs(inputs):
    f = lambda a: np.ascontiguousarray(np.asarray(a, dtype=np.float32))
    shared = {}
    for k in ("w_in", "w_o_attn", "w_o_ret", "w_out", "ffn_w1", "ffn_w3", "ffn_w2", "router_w",
              "moe_w1", "moe_w3", "moe_w2"):
        shared[k] = f(inputs[k])
    rows = [inputs["ln_emb_g"], inputs["ln_emb_b"],
            inputs["ln1_g"][0], inputs["ln1_b"][0], inputs["ln1_g"][1], inputs["ln1_b"][1],
            inputs["ln2_g"][0], inputs["ln2_b"][0], inputs["ln2_g"][1], inputs["ln2_b"][1]]
    shared["ln_tab"] = f(np.stack([np.broadcast_to(np.asarray(r, np.float32)[None, :], (128, D)) for r in rows]))
    shared["bgate"] = f(np.asarray(inputs["b_gate"]).reshape(DEPTH, 1, 2048))
    dec = np.stack([np.asarray(inputs["decay_fwd"], np.float32), np.asarray(inputs["decay_bwd"], np.float32)], axis=1)
    shared["dec_bc"] = f(np.broadcast_to(dec.reshape(1, -1), (128, DEPTH * 2 * 8)))
    pp = np.zeros((128, DEPTH, 2, 2), np.float32)
    for p in range(128):
        for hh in range(2):
            pp[p, :, :, hh] = dec[:, :, hh * 4 + p // 32]
    shared["dec_pp"] = f(pp.reshape(128, -1))
    shared["sink_bc"] = f(np.broadcast_to(np.asarray(inputs["sink_logits"], np.float32).reshape(1, -1), (128, DEPTH * 8)))
    shared["rb_bc"] = f(np.broadcast_to(np.asarray(inputs["router_b"], np.float32).reshape(1, -1), (128, NE)))
    shared.update(host_constants())
    return shared


def kernel(**inputs):
    stop = inputs.pop("_stop", "full")
    ncores = inputs.pop("_ncores", 8)
    if stop not in _CACHE:
        _CACHE[stop] = build_program(stop)
    nc = _CACHE[stop]
    shared = _host_inputs(inputs)
    x = np.asarray(inputs["x"], dtype=np.float32)
    in_maps = []
    for b in range(ncores):
        m = dict(shared)
        m["x"] = np.ascontiguousarray(x[b])
        in_maps.append(m)
    res = run_bass_kernel_spmd(nc, in_maps, core_ids=list(range(ncores)))
    out = np.stack([np.asarray(res.results[b]["y"], dtype=np.float32) for b in range(ncores)], axis=0)
    return out
```
